# Optimizing a Trainium2 kernel written in Bass

```python
import math
import jax
import jax.numpy as jnp
from jax import lax
import numpy as np

D_MODEL = 2048
BATCH = 32
SEQ = 256
DEPTH = 1
DEC_BATCH = 2
DEC_SEQ = 1024
PAST_LEN = 256

GRID_W = 64
D_HY = 1024
HY_ORDER = 2
HY_BANDS = 16
HY_EMB = 1 + 2 * HY_BANDS
HY_FILTER_HIDDEN = 64
HY_DECAY_FAST = 0.3
HY_DECAY_SLOW = 1.5
HY_DECAY_TARGET = 1e-2
N_RET_HEADS = 8
RET_HEAD_DIM = 128
D_RET = N_RET_HEADS * RET_HEAD_DIM
RET_CHUNK = 128
ROPE_BASE = 10000.0
ROPE_FREQS = RET_HEAD_DIM // 4
D_FF = 5632
N_IN = 3 * D_HY + 4 * D_RET + 2 * D_MODEL
RMS_EPS = 1e-6
GN_EPS = 1e-5
FILTER_EPS = 1e-6
F32 = jnp.float32

kernel_name = 'hybrid_hyena_retention_diffusion_step'


def rmsnorm(x, g):
    xf = x.astype(F32)
    y = xf * lax.rsqrt(jnp.mean(xf * xf, axis=-1, keepdims=True) + RMS_EPS)
    return (y * g.astype(F32)).astype(x.dtype)


def dwconv3(x, w):
    xp = jnp.pad(x, ((0, 0), (1, 1), (0, 0)))
    return xp[:, :-2] * w[0] + xp[:, 1:-1] * w[1] + xp[:, 2:] * w[2]


def rope_2d(L):
    ROWS = L // GRID_W
    row = jnp.repeat(jnp.arange(ROWS), GRID_W).astype(F32)
    col = jnp.tile(jnp.arange(GRID_W), ROWS).astype(F32)
    inv = ROPE_BASE ** (-jnp.arange(ROPE_FREQS, dtype=F32) / ROPE_FREQS)
    ang = jnp.concatenate([row[:, None] * inv, col[:, None] * inv], axis=-1)
    return jnp.cos(ang), jnp.sin(ang)


def apply_rope(x, cos, sin):
    half = x.shape[-1] // 2
    x1, x2 = x[..., :half], x[..., half:]
    c = cos[None, :, None, :]
    s = sin[None, :, None, :]
    return jnp.concatenate([x1 * c - x2 * s, x2 * c + x1 * s], axis=-1)


def hyena_filter_fft(L, w1, b1, w2, b2, w3, b3, freq, decay):
    n = jnp.arange(L, dtype=F32)
    t = n / L
    f = jnp.linspace(1e-4, HY_BANDS - 1, HY_BANDS, dtype=F32)
    w = 2.0 * math.pi * n / L
    z = jnp.concatenate([t[:, None], jnp.cos(w[:, None] * f), jnp.sin(w[:, None] * f)], axis=-1)
    fr = freq.astype(F32)
    h = jnp.sin(fr[0] * (z @ w1.astype(F32) + b1.astype(F32)))
    h = jnp.sin(fr[1] * (h @ w2.astype(F32) + b2.astype(F32)))
    h = (h @ w3.astype(F32) + b3.astype(F32)).reshape(L, 2, HY_ORDER, D_HY)
    h = h * jnp.exp(-t[:, None, None, None] * jnp.abs(decay.astype(F32))[None])
    h_fwd, h_bwd = h[:, 0], h[:, 1]
    k = jnp.concatenate([h_fwd, jnp.zeros((1, HY_ORDER, D_HY), F32), h_bwd[1:][::-1]], axis=0)
    k = k / (jnp.sum(jnp.abs(k), axis=0, keepdims=True) + FILTER_EPS)
    return jnp.fft.rfft(k, axis=0)


def fftconv(u, kf, bias):
    L = u.shape[1]
    uf = jnp.fft.rfft(u, n=2 * L, axis=1)
    y = jnp.fft.irfft(uf * kf[None], n=2 * L, axis=1)[:, :L]
    return y + u * bias


def ret_chunk_scan(q, k, v, gamma, state0):
    B, L, H, _ = q.shape
    DV = v.shape[-1]
    C = RET_CHUNK
    N = L // C
    log_g = jnp.log(gamma)
    idx = jnp.arange(C, dtype=F32)
    diff = idx[:, None] - idx[None, :]
    intra = jnp.where(diff[None] >= 0, jnp.exp(log_g[:, None, None] * jnp.maximum(diff, 0.0)[None]), 0.0)
    xi = jnp.exp(log_g[None, :] * (idx[:, None] + 1.0))
    zeta = jnp.exp(log_g[None, :] * (C - 1.0 - idx)[:, None])
    chunk_decay = jnp.exp(log_g * C)

    def chunks(a):
        return a.reshape(B, N, C, H, a.shape[-1]).transpose(1, 0, 2, 3, 4)

    def step(R, qkv):
        qc, kc, vc = qkv
        s = jnp.einsum('bihd,bjhd->bhij', qc, kc) * intra[None]
        o = jnp.einsum('bhij,bjhe->bihe', s, vc) + jnp.einsum('bihd,bhde->bihe', qc, R) * xi[None, :, :, None]
        R = R * chunk_decay[None, :, None, None] + jnp.einsum('bjhd,bjhe->bhde', kc * zeta[None, :, :, None], vc)
        return R, o

    R, o = lax.scan(step, state0, (chunks(q), chunks(k), chunks(v)))
    return o.transpose(1, 0, 2, 3, 4).reshape(B, L, H, DV), R


def token_mixer(h, rope, state0, w_in, hy_short_w, hy_w1, hy_b1, hy_w2, hy_b2, hy_w3, hy_b3,
                hy_freq, hy_decay, hy_bias, ret_decay_logit, ret_gn, w_br_hy, w_br_ret, w_out):
    dt = h.dtype
    B, L, _ = h.shape
    proj = h @ w_in
    o_ret = 3 * D_HY
    o_gate = o_ret + 4 * D_RET
    hy = dwconv3(proj[..., :o_ret], hy_short_w).astype(F32)
    x1, x2, v_h = jnp.split(hy, 3, axis=-1)
    kf = hyena_filter_fft(L, hy_w1, hy_b1, hy_w2, hy_b2, hy_w3, hy_b3, hy_freq, hy_decay)
    hb = hy_bias.astype(F32)
    z = x1 * fftconv(v_h, kf[:, 0], hb[0])
    y_hy = x2 * fftconv(z, kf[:, 1], hb[1])
    q, k, v, g = jnp.split(proj[..., o_ret:o_gate].astype(F32), 4, axis=-1)
    q = q.reshape(B, L, N_RET_HEADS, RET_HEAD_DIM)
    k = k.reshape(B, L, N_RET_HEADS, RET_HEAD_DIM)
    v = v.reshape(B, L, N_RET_HEADS, RET_HEAD_DIM)
    if rope is not None:
        q = apply_rope(q, rope[0], rope[1])
        k = apply_rope(k, rope[0], rope[1])
    k = k * RET_HEAD_DIM ** -0.5
    gamma = jax.nn.sigmoid(ret_decay_logit.astype(F32))
    s0 = state0.astype(F32)
    o_f, r_f = ret_chunk_scan(q, k, v, gamma[0], s0[:, 0])
    o_b, r_b = ret_chunk_scan(q[:, ::-1], k[:, ::-1], v[:, ::-1], gamma[1], s0[:, 1])
    o = o_f + o_b[:, ::-1]
    mu = jnp.mean(o, axis=-1, keepdims=True)
    var = jnp.mean(jnp.square(o - mu), axis=-1, keepdims=True)
    o = ((o - mu) * lax.rsqrt(var + GN_EPS)).reshape(B, L, D_RET) * ret_gn.astype(F32)
    y_ret = o * jax.nn.silu(g)
    g_hy, g_ret = jnp.split(jax.nn.sigmoid(proj[..., o_gate:].astype(F32)), 2, axis=-1)
    merged = (g_hy * (y_hy.astype(dt) @ w_br_hy).astype(F32)
              + g_ret * (y_ret.astype(dt) @ w_br_ret).astype(F32))
    out = merged.astype(dt) @ w_out
    return out.astype(dt), jnp.stack([r_f, r_b], axis=1)


def block(x, mod, rope, state0, norms, mixer_p, ffn_p):
    g_pre_m, g_post_m, g_pre_f, g_post_f = norms
    ffn_w_up, ffn_conv, ffn_w_down = ffn_p
    mod = mod.astype(x.dtype)
    shift_m, scale_m, gate_m, shift_f, scale_f, gate_f = jnp.split(mod, 6, axis=-1)
    h = rmsnorm(x, g_pre_m) * (1.0 + scale_m) + shift_m
    m, st = token_mixer(h, rope, state0, *mixer_p)
    x = x + (gate_m * rmsnorm(m, g_post_m)).astype(x.dtype)
    h = rmsnorm(x, g_pre_f) * (1.0 + scale_f) + shift_f
    u = dwconv3(h @ ffn_w_up, ffn_conv)
    a, b = jnp.split(u, 2, axis=-1)
    f = (jax.nn.gelu(a, approximate=True) * b) @ ffn_w_down
    x = x + (gate_f * rmsnorm(f, g_post_f)).astype(x.dtype)
    return x, st


def setup_inputs(seed: int = 0) -> dict:
    key = jax.random.key(seed)
    ks = jax.random.split(key, 40)

    def nrm(k, shape, scale):
        return jax.random.normal(k, shape, jnp.float32) * scale

    hy_rates = jnp.abs(jnp.linspace(math.log(HY_DECAY_TARGET) / HY_DECAY_FAST,
                                    math.log(HY_DECAY_TARGET) / HY_DECAY_SLOW, D_HY, dtype=jnp.float32))
    gam = 1.0 - 2.0 ** (-5.0 - jnp.arange(N_RET_HEADS, dtype=jnp.float32))
    ret_logit = jnp.log(gam) - jnp.log1p(-gam)
    FH = HY_FILTER_HIDDEN
    return {
        'x_prompt': nrm(ks[0], (BATCH, SEQ, D_MODEL), 1.0),
        'x_sample': nrm(ks[1], (DEC_BATCH, DEC_SEQ, D_MODEL), 1.0),
        'state_ret': nrm(ks[2], (DEC_BATCH, DEPTH, 2, N_RET_HEADS, RET_HEAD_DIM, RET_HEAD_DIM), 0.5),
        'c': nrm(ks[3], (DEC_BATCH, D_MODEL), 1.0),
        'c_ctx': nrm(ks[4], (D_MODEL,), 1.0),
        'w_ada': nrm(ks[5], (DEPTH, D_MODEL, 6 * D_MODEL), 0.5 * D_MODEL ** -0.5),
        'b_ada': nrm(ks[6], (DEPTH, 6 * D_MODEL), 0.02),
        'norm_pre_mix': 1.0 + nrm(ks[7], (DEPTH, D_MODEL), 0.05),
        'norm_post_mix': 1.0 + nrm(ks[8], (DEPTH, D_MODEL), 0.05),
        'norm_pre_ffn': 1.0 + nrm(ks[9], (DEPTH, D_MODEL), 0.05),
        'norm_post_ffn': 1.0 + nrm(ks[10], (DEPTH, D_MODEL), 0.05),
        'w_in': nrm(ks[11], (DEPTH, D_MODEL, N_IN), D_MODEL ** -0.5),
        'hy_short_w': nrm(ks[12], (DEPTH, 3, 3 * D_HY), 3 ** -0.5),
        'hy_w1': nrm(ks[13], (DEPTH, HY_EMB, FH), HY_EMB ** -0.5),
        'hy_b1': nrm(ks[14], (DEPTH, FH), 0.1),
        'hy_w2': nrm(ks[15], (DEPTH, FH, FH), FH ** -0.5),
        'hy_b2': nrm(ks[16], (DEPTH, FH), 0.1),
        'hy_w3': nrm(ks[17], (DEPTH, FH, 2 * HY_ORDER * D_HY), FH ** -0.5),
        'hy_b3': nrm(ks[18], (DEPTH, 2 * HY_ORDER * D_HY), 0.02),
        'hy_freq': 1.0 + nrm(ks[19], (DEPTH, 2, FH), 0.1),
        'hy_decay': hy_rates * (1.0 + nrm(ks[20], (DEPTH, 2, HY_ORDER, D_HY), 0.05)),
        'hy_bias': nrm(ks[21], (DEPTH, HY_ORDER, D_HY), 0.1),
        'ret_decay_logit': ret_logit + nrm(ks[22], (DEPTH, 2, N_RET_HEADS), 0.01),
        'ret_gn': 1.0 + nrm(ks[23], (DEPTH, D_RET), 0.05),
        'w_br_hy': nrm(ks[24], (DEPTH, D_HY, D_MODEL), D_HY ** -0.5),
        'w_br_ret': nrm(ks[25], (DEPTH, D_RET, D_MODEL), D_RET ** -0.5),
        'w_out': nrm(ks[26], (DEPTH, D_MODEL, D_MODEL), D_MODEL ** -0.5),
        'ffn_w_up': nrm(ks[27], (DEPTH, D_MODEL, 2 * D_FF), D_MODEL ** -0.5),
        'ffn_conv': nrm(ks[28], (DEPTH, 3, 2 * D_FF), 3 ** -0.5),
        'ffn_w_down': nrm(ks[29], (DEPTH, D_FF, D_MODEL), D_FF ** -0.5),
    }


def reference(x_prompt, x_sample, state_ret, c, c_ctx, w_ada, b_ada, norm_pre_mix, norm_post_mix,
              norm_pre_ffn, norm_post_ffn, w_in, hy_short_w, hy_w1, hy_b1, hy_w2, hy_b2, hy_w3, hy_b3,
              hy_freq, hy_decay, hy_bias, ret_decay_logit, ret_gn, w_br_hy, w_br_ret, w_out,
              ffn_w_up, ffn_conv, ffn_w_down):
    zero_state = jnp.zeros((x_prompt.shape[0], 2, N_RET_HEADS, RET_HEAD_DIM, RET_HEAD_DIM), F32)
    rope = rope_2d(x_sample.shape[1])
    x_p = x_prompt
    x_s = x_sample
    states = []
    for l in range(DEPTH):
        mixer_p = (w_in[l], hy_short_w[l], hy_w1[l], hy_b1[l], hy_w2[l], hy_b2[l], hy_w3[l], hy_b3[l],
                   hy_freq[l], hy_decay[l], hy_bias[l], ret_decay_logit[l], ret_gn[l],
                   w_br_hy[l], w_br_ret[l], w_out[l])
        norms = (norm_pre_mix[l], norm_post_mix[l], norm_pre_ffn[l], norm_post_ffn[l])
        ffn_p = (ffn_w_up[l], ffn_conv[l], ffn_w_down[l])
        mod_ctx = (jax.nn.silu(c_ctx) @ w_ada[l] + b_ada[l])[None, None, :]
        mod_lat = (jax.nn.silu(c) @ w_ada[l] + b_ada[l])[:, None, :]
        x_p, st = block(x_p, mod_ctx, None, zero_state, norms, mixer_p, ffn_p)
        x_s, _ = block(x_s, mod_lat, rope, state_ret[:, l], norms, mixer_p, ffn_p)
        states.append(st)
    new_state_ret = jnp.stack(states, axis=1)
    return (x_p, x_s, new_state_ret)
```

```python
import math
from contextlib import ExitStack

import numpy as np
import concourse.bass as bass
import concourse.mybir as mybir
from concourse.bass_utils import run_bass_kernel_spmd

F32 = mybir.dt.float32
BF16 = mybir.dt.bfloat16
AF = mybir.ActivationFunctionType
ALU = mybir.AluOpType

D = 2048
KC = 16
DFF = 5632
NJ = 44
LP = 256
LS = 1024
WIN = 258
SLOT = 4096
NSLOT = 3
RMS_EPS = 1e-6
GN_EPS = 1e-5
FILTER_EPS = 1e-6
MAGIC = 12582912.0
TWO_PI = 2.0 * math.pi


class Buf:
    __slots__ = ("name", "w", "r")

    def __init__(self, name=""):
        self.name = name
        self.w = None
        self.r = {}


class Sched:
    ROLL = 16000
    NDSEM = 12

    def __init__(self, nc, es):
        self.nc = nc
        self.es = es
        self.eng = {"pe": nc.tensor, "dve": nc.vector, "act": nc.scalar, "pool": nc.gpsimd, "sp": nc.sync}
        self.sem = {}
        self.cnt = {}
        self.seen = {k: {} for k in self.eng}
        self.owner = {}
        self.pend = {k: [] for k in self.eng}
        self.nsem = 0
        self.keep = []
        for k in self.eng:
            self._newsem(k)
        self.dq = {q: {"sems": [], "uses": [], "next": 0} for q in ("sp", "act", "pool")}
        self.out_toks = []

    def _mk(self, name):
        self.nsem += 1
        s = self.es.enter_context(self.nc.semaphore(f"{name}_{self.nsem}"))
        self.keep.append(s)
        return s

    def _newsem(self, k):
        self.sem[k] = self._mk("e" + k)
        self.owner[id(self.sem[k])] = k
        self.cnt[k] = 0

    def _wait(self, e, tok):
        sem, val = tok
        key = id(sem)
        if e == "pe" and self.owner.get(key) == "pe":
            return
        if self.seen[e].get(key, 0) >= val:
            return
        self.seen[e][key] = val
        self.eng[e].wait_ge(sem, val)

    def _deps(self, e, reads, writes):
        for b in reads:
            if b.w is not None:
                self._wait(e, b.w)
        for b in writes:
            if b.w is not None:
                self._wait(e, b.w)
            for t in list(b.r.values()):
                self._wait(e, t)

    def _commit(self, tok, reads, writes):
        key = id(tok[0])
        for b in reads:
            b.r[key] = tok
        for b in writes:
            b.w = tok
            b.r = {}

    def op(self, e, fn, r=(), w=(), sig=True):
        for oe, p in self.pend.items():
            assert oe == e or not p, f"pending unsignaled ops on {oe} while emitting on {e}"
        self._deps(e, r, w)
        inst = fn(self.eng[e])
        if not sig:
            self.pend[e].append((r, w))
            return None
        if self.cnt[e] >= self.ROLL:
            self._newsem(e)
        self.cnt[e] += 1
        inst.then_inc(self.sem[e], 1)
        tok = (self.sem[e], self.cnt[e])
        for (pr, pw) in self.pend[e]:
            self._commit(tok, pr, pw)
        self.pend[e] = []
        self._commit(tok, r, w)
        return tok

    def dma(self, q, out, in_, r=(), w=(), is_out=False, **kw):
        for oe, p in self.pend.items():
            assert not p, f"pending unsignaled ops on {oe} while emitting dma on {q}"
        self._deps(q, r, w)
        d = self.dq[q]
        if len(d["sems"]) < self.NDSEM:
            d["sems"].append(self._mk("d" + q))
            d["uses"].append(0)
            i = len(d["sems"]) - 1
        else:
            i = d["next"] % self.NDSEM
        d["next"] += 1
        if d["uses"][i] >= 900:
            d["sems"][i] = self._mk("d" + q)
            d["uses"][i] = 0
        sem = d["sems"][i]
        if d["uses"][i] > 0:
            self._wait(q, (sem, 16 * d["uses"][i]))
        d["uses"][i] += 1
        self.eng[q].dma_start(out=out, in_=in_, **kw).then_inc(sem, 16)
        tok = (sem, 16 * d["uses"][i])
        self._commit(tok, r, w)
        if is_out:
            self.out_toks.append(tok)
        return tok

    def barrier(self, engines=("pe", "dve", "act", "sp")):
        toks = []
        for e in engines:
            assert not self.pend[e]
            if self.cnt[e] > 0:
                toks.append((self.sem[e], self.cnt[e]))
        for q in ("sp", "act"):
            d = self.dq[q]
            for s, u in zip(d["sems"], d["uses"]):
                if u > 0:
                    toks.append((s, 16 * u))
        self.last_barrier = toks
        for e in engines:
            for t in toks:
                if self.owner.get(id(t[0])) == e:
                    continue
                self._wait(e, t)

    def join(self, e):
        for t in getattr(self, "last_barrier", []):
            self._wait(e, t)

    def finish(self):
        for t in self.out_toks:
            self._wait("sp", t)


def _ktile(w):
    n = w.shape[1]
    return np.ascontiguousarray(w.reshape(KC, 128, n).transpose(1, 0, 2)).reshape(128, KC * n)


def _fm(v, nchunk):
    return np.ascontiguousarray(v.reshape(nchunk, 128).T)


def _dft_consts(L):
    n = np.arange(L, dtype=np.float64)
    ang = np.pi * np.outer(n, n) / L
    C = np.cos(ang).astype(np.float32)
    S = (-np.sin(ang)).astype(np.float32)
    nt = L // 128

    def tile(m):
        return np.ascontiguousarray(m.reshape(nt, 128, L).transpose(1, 0, 2)).reshape(128, nt * L)

    nyq = ((-1.0) ** n).astype(np.float32)
    nyq_col = np.ascontiguousarray(nyq.reshape(nt, 128).T)
    nyq_row = nyq.reshape(1, L)
    wf = np.full(L, 1.0 / L, np.float32)
    wf[0] = 1.0 / (2 * L)
    wf_col = np.ascontiguousarray(wf.reshape(nt, 128).T)
    negt = np.ascontiguousarray((-(n / L)).astype(np.float32).reshape(nt, 128).T)
    nf = np.arange(L, dtype=np.float32)
    t = nf / np.float32(L)
    f = np.linspace(1e-4, 16 - 1, 16, dtype=np.float32)
    w = (np.float32(2.0 * math.pi) * nf / np.float32(L)).astype(np.float32)
    z = np.concatenate([t[:, None], np.cos(w[:, None] * f), np.sin(w[:, None] * f)], axis=-1).astype(np.float32)
    zT = np.ascontiguousarray(z.T)
    return dict(C=tile(C), S=tile(S), nyq_col=nyq_col, nyq_row=nyq_row, wf=wf_col, negt=negt, zT=zT)


def _rope_consts():
    L = LS
    rows = L // 64
    row = np.repeat(np.arange(rows), 64).astype(np.float32)
    col = np.tile(np.arange(64), rows).astype(np.float32)
    inv = (np.float32(10000.0) ** (-np.arange(32, dtype=np.float32) / np.float32(32))).astype(np.float32)
    ang = np.concatenate([row[:, None] * inv, col[:, None] * inv], axis=-1)
    c = np.cos(ang).astype(np.float32).T
    s = np.sin(ang).astype(np.float32).T
    CC = np.concatenate([c, c], axis=0)
    SS = np.concatenate([-s, s], axis=0)
    return np.ascontiguousarray(CC), np.ascontiguousarray(SS)


def _ret_consts():
    i = np.arange(128, dtype=np.float32)
    J, I = np.meshgrid(i, i, indexing="ij")
    dpos = np.maximum(I - J, 0.0)
    dneg = np.maximum(J - I, 0.0)
    mge = (I >= J).astype(np.float32)
    mle = (J >= I).astype(np.float32)
    iota1 = np.broadcast_to(i[None, :] + 1.0, (128, 128))
    cmi = np.broadcast_to(128.0 - i[None, :], (128, 128))
    rc = np.stack([dpos, dneg, mge, mle, iota1, cmi], axis=1).astype(np.float32)
    pc = np.stack([127.0 - i, i, np.full(128, 128.0, np.float32)], axis=1).astype(np.float32)
    return np.ascontiguousarray(rc), np.ascontiguousarray(pc)


def host_prep(inp):
    g = lambda k: np.asarray(inp[k], dtype=np.float32)
    w_in = g("w_in")[0]
    sh = {}
    sh["wada"] = np.stack([_ktile(g("w_ada")[0][:, b * 512:(b + 1) * 512]).reshape(128, 2, 8 * 512)[:, h]
                           for b in range(24) for h in range(2)])
    sh["whyA"] = np.stack([_ktile(np.concatenate([w_in[:, cb * 128:(cb + 1) * 128],
                                                  w_in[:, 1024 + cb * 128:1024 + (cb + 1) * 128]], axis=1))
                           for cb in range(8)])
    sh["whyB"] = np.stack([_ktile(w_in[:, 2048 + cb * 128:2048 + (cb + 1) * 128]) for cb in range(8)])
    o = 3072
    sh["wqk"] = np.stack([_ktile(np.concatenate([w_in[:, o + h * 128:o + (h + 1) * 128],
                                                 w_in[:, o + 1024 + h * 128:o + 1024 + (h + 1) * 128]], axis=1))
                          for h in range(8)])
    sh["wvg"] = np.stack([_ktile(np.concatenate([w_in[:, o + 2048 + h * 128:o + 2048 + (h + 1) * 128],
                                                 w_in[:, o + 3072 + h * 128:o + 3072 + (h + 1) * 128]], axis=1))
                          for h in range(8)])
    og = 7168
    wbh = g("w_br_hy")[0]
    wbr = g("w_br_ret")[0]

    def brt(w, n):
        return np.ascontiguousarray(w[:, n * 128:(n + 1) * 128].reshape(8, 128, 128).transpose(1, 0, 2)).reshape(128, 1024)

    sh["wmg"] = np.stack([_ktile(np.concatenate([w_in[:, og + n * 128:og + (n + 1) * 128],
                                                 w_in[:, og + 2048 + n * 128:og + 2048 + (n + 1) * 128]], axis=1))
                          for n in range(16)])
    sh["wmb"] = np.stack([np.concatenate([brt(wbh, n), brt(wbr, n)], axis=1) for n in range(16)])
    wo = g("w_out")[0]
    sh["wout"] = np.stack([_ktile(wo[:, cc * 512:(cc + 1) * 512]).reshape(128, 2, 8 * 512)[:, h]
                           for cc in range(4) for h in range(2)])
    wu = g("ffn_w_up")[0]
    sh["wup"] = np.stack([_ktile(np.concatenate([wu[:, j * 128:(j + 1) * 128],
                                                 wu[:, DFF + j * 128:DFF + (j + 1) * 128]], axis=1))
                          for j in range(NJ)])
    wd = g("ffn_w_down")[0]
    wdn = np.zeros((24, 128, 4096), np.float32)
    for cc in range(4):
        for jg in range(6):
            nj = 8 if jg < 5 else 4
            blk = wd[jg * 8 * 128:(jg * 8 + nj) * 128, cc * 512:(cc + 1) * 512].reshape(nj, 128, 512).transpose(1, 0, 2)
            wdn[cc * 6 + jg, :, 0:nj * 512] = blk.reshape(128, nj * 512)
    sh["wdn"] = wdn
    sh = {k: np.ascontiguousarray(v, dtype=np.float32) for k, v in sh.items()}
    sh["badaT"] = _fm(g("b_ada")[0], 96)
    sh["gnorm"] = np.ascontiguousarray(np.stack([_fm(g("norm_pre_mix")[0], 16), _fm(g("norm_pre_ffn")[0], 16),
                                                 _fm(g("norm_post_mix")[0], 16), _fm(g("norm_post_ffn")[0], 16)], axis=1))
    hs = g("hy_short_w")[0]
    sh["hsw"] = np.ascontiguousarray(hs.reshape(3, 24, 128).transpose(2, 1, 0))
    sh["hbias"] = np.ascontiguousarray(g("hy_bias")[0].reshape(2, 8, 128).transpose(2, 0, 1))
    fc = g("ffn_conv")[0]
    sh["fcw"] = np.ascontiguousarray(fc.reshape(3, 88, 128).transpose(2, 1, 0))
    sh["retgn"] = g("ret_gn")[0].reshape(1, 1024)
    sh["retlogit"] = g("ret_decay_logit")[0].reshape(1, 16)
    sh["hyw1"] = g("hy_w1")[0]
    sh["hyw2"] = g("hy_w2")[0]
    sh["hyb12"] = np.ascontiguousarray(np.stack([g("hy_b1")[0], g("hy_b2")[0]], axis=1))
    sh["hyfreq"] = np.ascontiguousarray(g("hy_freq")[0].T)
    w3 = g("hy_w3")[0]
    b3 = g("hy_b3")[0]
    dec = g("hy_decay")[0].reshape(4096)
    w3b = np.zeros((8, 65, 512), np.float32)
    decb = np.zeros((8, 512), np.float32)
    for cb in range(8):
        cols = np.concatenate([np.arange(128) + d_ * 2048 + o_ * 1024 + cb * 128 for d_ in range(2) for o_ in range(2)])
        w3b[cb, :64] = w3[:, cols]
        w3b[cb, 64] = b3[cols]
        decb[cb] = dec[cols]
    sh["w3b"] = w3b
    sh["decb"] = decb
    sh["ident"] = np.eye(128, dtype=np.float32)
    ps = np.zeros((128, 128), np.float32)
    for dp in range(128):
        ps[(dp + 64) % 128, dp] = 1.0
    sh["pswap"] = ps
    for L, tag in ((LP, "p"), (LS, "s")):
        dc = _dft_consts(L)
        for k, v in dc.items():
            sh[f"dft{tag}_{k}"] = v
    sh["ropeC"], sh["ropeS"] = _rope_consts()
    sh["retc"], sh["retpc"] = _ret_consts()

    xs = g("x_sample")
    xp = g("x_prompt")
    st = g("state_ret")
    c = g("c")
    cctx = g("c_ctx")
    per = []
    for core in range(8):
        s, j = core // 4, core % 4
        m = {}
        m["xs_full"] = xs[s]
        win = np.zeros((WIN, D), np.float32)
        sel = np.zeros((LS, WIN), np.float32)
        for w_ in range(WIN):
            t = 256 * j - 1 + w_
            if 0 <= t < LS:
                win[w_] = xs[s, t]
                sel[t, w_] = 1.0
        m["xs_win"] = win
        m["sel"] = np.ascontiguousarray(sel.reshape(8, 128, WIN).transpose(1, 0, 2)).reshape(128, 8 * WIN)
        m["hmask"] = np.ascontiguousarray(np.broadcast_to(
            np.array([[1.0 if j > 0 else 0.0, 1.0 if j < 3 else 0.0]], np.float32), (128, 2)))
        m["xp"] = np.ascontiguousarray(xp[4 * core:4 * core + 4].reshape(4 * LP, D))
        m["cT"] = np.ascontiguousarray(np.stack([_fm(c[s], 16), _fm(cctx, 16)], axis=2))
        m["state0"] = np.ascontiguousarray(st[s, 0])
        per.append(m)
    return sh, per


SHAPES_SHARED = {
    "wada": [48, 128, 4096], "whyA": [8, 128, 4096], "whyB": [8, 128, 2048], "wqk": [8, 128, 4096], "wvg": [8, 128, 4096],
    "wmg": [16, 128, 4096], "wmb": [16, 128, 2048], "wout": [8, 128, 4096], "wup": [44, 128, 4096], "wdn": [24, 128, 4096],
    "badaT": [128, 96], "gnorm": [128, 4, 16], "hsw": [128, 24, 3], "hbias": [128, 2, 8], "fcw": [128, 88, 3],
    "retgn": [1, 1024], "retlogit": [1, 16], "hyw1": [33, 64], "hyw2": [64, 64], "hyb12": [64, 2],
    "hyfreq": [64, 2], "w3b": [8, 65, 512], "decb": [8, 512], "ident": [128, 128], "pswap": [128, 128],
    "ropeC": [128, LS], "ropeS": [128, LS], "retc": [128, 6, 128], "retpc": [128, 3],
}
for _L, _tag in ((LP, "p"), (LS, "s")):
    _nt = _L // 128
    SHAPES_SHARED.update({f"dft{_tag}_C": [128, _nt * _L], f"dft{_tag}_S": [128, _nt * _L],
                          f"dft{_tag}_nyq_col": [128, _nt], f"dft{_tag}_nyq_row": [1, _L],
                          f"dft{_tag}_wf": [128, _nt], f"dft{_tag}_negt": [128, _nt], f"dft{_tag}_zT": [33, _L]})
SHAPES_CORE = {"xs_full": [LS, D], "xs_win": [WIN, D], "sel": [128, 8 * WIN], "hmask": [128, 2],
               "xp": [4 * LP, D], "cT": [128, 16, 2], "state0": [2, 8, 128, 128]}


def build(cfg=None):
    cfg = cfg or {}
    passes = cfg.get("passes", [0, 1, 2])
    dumps = cfg.get("dump", {})
    stop_after = cfg.get("stop_after", None)
    rstop = cfg.get("ret_stop", 99)
    nc = bass.Bass("TRN2", target_bir_lowering=False)
    din = {}
    for k, shp in list(SHAPES_SHARED.items()) + list(SHAPES_CORE.items()):
        din[k] = nc.dram_tensor(k, list(shp), F32, kind="ExternalInput").ap()
    yp = nc.dram_tensor("yp", [4 * LP, D], F32, kind="ExternalOutput").ap()
    ys = nc.dram_tensor("ys", [256, D], F32, kind="ExternalOutput").ap()
    stout = nc.dram_tensor("stout", [4, 2, 8, 128, 128], F32, kind="ExternalOutput").ap()
    dbg_out = {k: nc.dram_tensor("dbg_" + k, list(shp), F32, kind="ExternalOutput").ap() for k, shp in dumps.items()}

    with ExitStack() as es:
        S = Sched(nc, es)
        cnt = [0]

        def sb(shape, dt, scope=None, name=None, side=None):
            cnt[0] += 1
            kw = {"side": side} if side else {}
            return (scope or es).enter_context(nc.sbuf_tensor(f"{name or 't'}{cnt[0]}", list(shape), dt, **kw))

        pbank = [es.enter_context(nc.psum_tensor(f"pb{i}", [128, 512], F32)) for i in range(8)]
        pbq = [[Buf(f"pb{i}q{q}") for q in range(4)] for i in range(8)]

        def PB(i, c0=0, c1=512):
            return [pbq[i][0]]

        def pbf(i):
            return pbank[i][:].bitcast(BF16)

        def PBb(i, c0=0, c1=1024):
            return [pbq[i][0]]

        ring = [sb([128, SLOT], BF16, name="ring") for _ in range(NSLOT)]
        ringb = [Buf(f"ring{i}") for i in range(NSLOT)]
        wplan = []

        def plan_pass(first):
            for h in range(8):
                wplan.append(("wqk", h, 4096))
                wplan.append(("wvg", h, 4096))
                if first and h >= 1:
                    for i in range(4):
                        wplan.append(("wada", 16 + (h - 1) * 4 + i, 4096))
            if first:
                for i in range(4):
                    wplan.append(("wada", 16 + 7 * 4 + i, 4096))
            for cb in range(8):
                wplan.append(("whyA", cb, 4096))
                wplan.append(("whyB", cb, 2048))
            for n in range(16):
                wplan.append(("wmg", n, 4096))
                wplan.append(("wmb", n, 2048))
            for i in range(8):
                wplan.append(("wout", i, 4096))
            for j in range(NJ):
                wplan.append(("wup", j, 4096))
            for i in range(24):
                wplan.append(("wdn", i, 4096 if i % 6 < 5 else 2048))

        for i in range(16):
            wplan.append(("wada", i, 4096))
        for pidx, _ in enumerate(passes):
            plan_pass(pidx == 0)
        wstate = {"issued": 0, "taken": 0}

        def w_issue():
            i = wstate["issued"]
            if i >= len(wplan):
                return
            name, idx, n = wplan[i]
            S.dma("pool", ring[i % NSLOT][:, 0:n], din[name][idx][:, 0:n], w=[ringb[i % NSLOT]])
            wstate["issued"] += 1

        def w_take(name, idx, prev_done=True):
            i = wstate["taken"]
            assert wplan[i][0] == name and wplan[i][1] == idx, (wplan[i], name, idx)
            while wstate["issued"] < min(len(wplan), i + NSLOT - (0 if prev_done else 1)):
                w_issue()
            wstate["taken"] += 1
            return ring[i % NSLOT], ringb[i % NSLOT]

        dbg_tmp = {k: sb(list(shp), F32, name="dbg", side="right") for k, shp in dumps.items()}

        def dump(name, ap, bufs):
            if name in dbg_out:
                tmp = dbg_tmp[name]
                tb = Buf("dbg" + name)
                S.op("dve", lambda e: e.tensor_copy(tmp[:], ap), r=bufs, w=[tb])
                S.dma("sp", dbg_out[name], tmp[:], r=[tb], w=[Buf()], is_out=True)

        ident_f = sb([128, 128], F32); ident_b = sb([128, 128], BF16)
        ones_f = sb([128, 128], F32); ones_b = sb([128, 128], BF16)
        pswap = sb([128, 128], F32)
        epsr = sb([128, 1], F32); epsg = sb([128, 1], F32)
        badaT = sb([128, 96], F32); gnorm = sb([128, 4, 16], F32)
        hsw = sb([128, 24, 3], F32); hbias = sb([128, 2, 8], F32); fcw = sb([128, 88, 3], F32)
        gn_bc = sb([128, 1024], F32)
        modT = sb([128, 96, 2], F32)
        cT = sb([128, 16, 2], F32); sT = sb([128, 16, 2], BF16)
        retc = sb([128, 6, 128], F32); retpc = sb([128, 3], F32)
        lg = sb([128, 16], F32)
        msum = sb([128, 8, 128], F32)
        xi_bc = sb([128, 16, 128], F32)
        zcol = sb([128, 16], F32)
        cdcol = sb([128, 16], F32)
        hmask = sb([128, 2], F32)
        CB = Buf("consts")

        CLOAD = []

        def _cl():
            b_ = Buf(f"cload{len(CLOAD)}")
            CLOAD.append(b_)
            return b_

        S.dma("sp", ident_f[:], din["ident"], w=[_cl()])
        S.dma("sp", pswap[:], din["pswap"], w=[_cl()])
        S.dma("sp", badaT[:], din["badaT"], w=[_cl()])
        S.dma("sp", gnorm[:], din["gnorm"], w=[_cl()])
        S.dma("sp", hsw[:], din["hsw"], w=[_cl()])
        S.dma("sp", hbias[:], din["hbias"], w=[_cl()])
        S.dma("sp", fcw[:], din["fcw"], w=[_cl()])
        S.dma("sp", gn_bc[:], din["retgn"].partition_broadcast(128), w=[_cl()])
        S.dma("sp", lg[:], din["retlogit"].partition_broadcast(128), w=[_cl()])
        S.dma("sp", cT[:], din["cT"], w=[_cl()])
        S.dma("sp", retc[:], din["retc"], w=[_cl()])
        S.dma("sp", retpc[:], din["retpc"], w=[_cl()])
        S.dma("sp", hmask[:], din["hmask"], w=[_cl()])
        S.op("dve", lambda e: e.memset(epsr[:], RMS_EPS), r=CLOAD, w=[CB])
        S.op("dve", lambda e: e.tensor_copy(ident_b[:], ident_f[:]), r=[CB], w=[CB])
        S.op("dve", lambda e: e.memset(ones_f[:], 1.0), w=[CB])
        S.op("dve", lambda e: e.memset(ones_b[:], 1.0), w=[CB])

        sf = sb([128, 16, 2], F32)
        S.op("act", lambda e: e.activation(sf[:], cT[:], AF.Silu), r=[CB], w=[CB])
        S.op("dve", lambda e: e.tensor_copy(sT[:], sf[:]), r=[CB], w=[CB])
        S.op("act", lambda e: e.activation(lg[:], lg[:], AF.Exp, scale=-1.0), r=[CB], w=[CB])
        S.op("dve", lambda e: e.tensor_scalar(lg[:], lg[:], 1.0, None, ALU.add), r=[CB], w=[CB])
        S.op("act", lambda e: e.activation(lg[:], lg[:], AF.Ln), r=[CB], w=[CB])
        S.op("dve", lambda e: e.tensor_scalar(lg[:], lg[:], -1.0, None, ALU.mult), r=[CB], w=[CB])
        with ExitStack() as ph:
            e1s = [sb([128, 128], F32, ph) for _ in range(2)]; e2s = [sb([128, 128], F32, ph) for _ in range(2)]
            TB1 = [Buf("rt1a"), Buf("rt1b")]; TB2 = [Buf("rt2a"), Buf("rt2b")]
            RCA = Buf("retc_act"); RCD = Buf("retc_dve")
            for h in range(8):
                lf = lg[:, h:h + 1]
                lb = lg[:, 8 + h:9 + h]
                e1, e2, T1, T2 = e1s[h % 2], e2s[h % 2], TB1[h % 2], TB2[h % 2]
                S.op("act", lambda e: e.activation(e1[:], retc[:, 0, :], AF.Exp, scale=lf), r=[CB], w=[T1])
                S.op("dve", lambda e: e.tensor_tensor(e1[:], e1[:], retc[:, 2, :], ALU.mult), r=[CB, T1], w=[T1])
                S.op("act", lambda e: e.activation(e2[:], retc[:, 1, :], AF.Exp, scale=lb), r=[CB], w=[T2])
                S.op("dve", lambda e: e.tensor_tensor(e2[:], e2[:], retc[:, 3, :], ALU.mult), r=[CB, T2], w=[T2])
                S.op("dve", lambda e: e.tensor_tensor(msum[:, h, :], e1[:], e2[:], ALU.add), r=[T1, T2], w=[RCD])
                S.op("act", lambda e: e.activation(xi_bc[:, h, :], retc[:, 4, :], AF.Exp, scale=lf), r=[CB], w=[RCA])
                S.op("act", lambda e: e.activation(xi_bc[:, 8 + h, :], retc[:, 5, :], AF.Exp, scale=lb), r=[CB], w=[RCA])
                S.op("act", lambda e: e.activation(zcol[:, h:h + 1], retpc[:, 0:1], AF.Exp, scale=lf), r=[CB], w=[RCA])
                S.op("act", lambda e: e.activation(zcol[:, 8 + h:9 + h], retpc[:, 1:2], AF.Exp, scale=lb), r=[CB], w=[RCA])
                S.op("act", lambda e: e.activation(cdcol[:, h:h + 1], retpc[:, 2:3], AF.Exp, scale=lf), r=[CB], w=[RCA])
                S.op("act", lambda e: e.activation(cdcol[:, 8 + h:9 + h], retpc[:, 2:3], AF.Exp, scale=lb), r=[CB], w=[RCA])
            S.op("dve", lambda e: e.memset(epsg[:], GN_EPS), r=[RCA, RCD], w=[CB])
            S.barrier()

        MODB = Buf("modT_late")
        late_mod = [8]

        def mod_cblk(cblk, mrow_t, mrow_b, bacc, btr, outbuf):
            for half in range(2):
                wt, wb = w_take("wada", cblk * 2 + half)
                wv = wt[:, 0:4096].rearrange("p (k n) -> p k n", n=512)
                for k in range(8):
                    kc = half * 8 + k
                    S.op("pe", lambda e: e.matmul(pbank[bacc][0:2, :], sT[:, kc, :], wv[:, k, :],
                                                  start=(kc == 0), stop=(kc == 15)),
                         r=[CB, wb], w=PB(bacc), sig=(k == 7))
            S.op("act", lambda e: e.activation(mrow_t[:], pbank[bacc][0:2, :], AF.Copy), r=PB(bacc), w=[mrow_b])
            for q in range(4):
                S.op("pe", lambda e: e.matmul(pbank[btr][:, q * 2:q * 2 + 2], mrow_t[0:2, q * 128:(q + 1) * 128],
                                              ident_f[0:2, 0:2], start=True, stop=True),
                     r=[mrow_b, CB], w=PB(btr), sig=(q == 3))
            S.op("dve", lambda e: e.tensor_tensor(
                modT[:, cblk * 4:(cblk + 1) * 4, :],
                pbank[btr][:, 0:8].rearrange("p (q v) -> p q v", v=2),
                badaT[:, cblk * 4:(cblk + 1) * 4].unsqueeze(2).to_broadcast([128, 4, 2]), ALU.add),
                r=PB(btr) + [CB], w=[outbuf])

        with ExitStack() as ph:
            mrow = [sb([2, 512], F32, ph) for _ in range(2)]
            MB = [Buf("mrow0"), Buf("mrow1")]
            zb = sb([128, 512], BF16, ph)
            ZBB = Buf("zb")
            S.op("dve", lambda e: e.memset(zb[:], 0.0), w=[ZBB])
            for i_ in range(8):
                S.op("pe", lambda e: e.matmul(pbank[i_][:, :], zb[:, 0:128], zb[:], start=True, stop=True), r=[ZBB], w=PB(i_))
            for cblk in range(8):
                mod_cblk(cblk, mrow[cblk % 2], MB[cblk % 2], cblk % 2, 2 + cblk % 2, CB)
            S.barrier()

        def rstd_from_ss(ss_ap, out_ap, bufs_r, bufs_w, eps_ap, scale):
            S.op("act", lambda e: e.activation(out_ap, ss_ap, AF.Sqrt, bias=eps_ap, scale=scale), r=bufs_r + [CB], w=bufs_w)
            S.op("dve", lambda e: e.reciprocal(out_ap, out_ap), r=bufs_w, w=bufs_w)

        def norm_T(ph, x_rows, dstT, dstB, gs_ap, sh_ap, vecB, col0=0):
            need_x = any(sbuf is None for (_, _, sbuf) in x_rows)
            xin = [sb([128, D], F32, ph) for _ in range(2)] if need_x else None
            xn = [sb([128, D], BF16, ph) for _ in range(2)]
            junk = sb([128, D], BF16, ph)
            ssx = sb([128, 8], F32, ph)
            XB = [Buf("xin0"), Buf("xin1")]
            NB = [Buf("xn0"), Buf("xn1")]
            JB = Buf("junk")
            SSB = Buf("ssx")
            S.op("dve", lambda e: e.memset(ssx[:], 0.0), w=[SSB])
            SST = [Buf(f"ss{i}") for i in range(len(x_rows))]
            cols = []
            c_ = col0
            for (_, rows, _) in x_rows:
                cols.append(c_)
                c_ += rows

            def front(ti):
                src, rows, srcbuf = x_rows[ti]
                b = ti % 2
                if srcbuf is None:
                    S.dma("sp", xin[b][0:rows, :], src, w=[XB[b]])
                    xa, xb_ = xin[b], [XB[b]]
                else:
                    xa, xb_ = src, [srcbuf]
                ssc = ssx[0:rows, ti % 8:ti % 8 + 1]
                S.op("act", lambda e: e.activation(junk[0:rows, :], xa[0:rows, :], AF.Square, accum_out=ssc), r=xb_ + [SSB], w=[JB, SST[ti]])
                rstd_from_ss(ssc, ssc, [SST[ti]], [SST[ti]], epsr[0:rows, :], 1.0 / D)
                S.op("act", lambda e: e.activation(xn[b][0:rows, :], xa[0:rows, :], AF.Copy, scale=ssc), r=xb_ + [SST[ti]], w=[NB[b]])

            def trans(ti):
                _, rows, _ = x_rows[ti]
                b = ti % 2
                for half in range(2):
                    bank = (6 if ti % 2 == 0 else 4) + half
                    for k in range(8):
                        kc = half * 8 + k
                        S.op("pe", lambda e: e.transpose(pbf(bank)[:, k * 128:k * 128 + rows], xn[b][0:rows, kc * 128:(kc + 1) * 128],
                                                         ident_b[0:rows, 0:rows]),
                             r=[NB[b], CB], w=PB(bank), sig=(k == 7))

            def evac(ti):
                _, rows, _ = x_rows[ti]
                col = cols[ti]
                for half in range(2):
                    bank = (6 if ti % 2 == 0 else 4) + half
                    for k in range(8):
                        kc = half * 8 + k
                        S.op("dve", lambda e: e.tensor_scalar(dstT[:, kc, col:col + rows], pbf(bank)[:, k * 128:k * 128 + rows],
                                                              gs_ap[:, kc:kc + 1], sh_ap(kc), ALU.mult, ALU.add),
                             r=PB(bank) + [vecB], w=[dstB])

            n_ = len(x_rows)
            front(0)
            trans(0)
            for ti in range(1, n_):
                front(ti)
                evac(ti - 1)
                trans(ti)
            evac(n_ - 1)

        def conv3(out_ap, raw_ap, wcol, nseg, seglen, rB, wB, raw_is_psum=False):
            S.op("act", lambda e: e.activation(out_ap, raw_ap, AF.Copy, scale=wcol[:, 1:2]), r=rB + [CB], w=wB)
            o3 = out_ap.rearrange("p (s l) -> p s l", l=seglen)
            r3 = raw_ap.rearrange("p (s l) -> p s l", l=seglen)
            S.op("dve", lambda e: e.scalar_tensor_tensor(o3[:, :, 1:seglen], r3[:, :, 0:seglen - 1], wcol[:, 0:1],
                                                         o3[:, :, 1:seglen], ALU.mult, ALU.add), r=rB + wB + [CB], w=wB)
            S.op("dve", lambda e: e.scalar_tensor_tensor(o3[:, :, 0:seglen - 1], r3[:, :, 1:seglen], wcol[:, 2:3],
                                                         o3[:, :, 0:seglen - 1], ALU.mult, ALU.add), r=rB + wB + [CB], w=wB)

        def drive(ga, gb):
            a_done = ga is None
            b_done = gb is None
            while not (a_done and b_done):
                if not b_done:
                    try:
                        next(gb)
                    except StopIteration:
                        b_done = True
                if not a_done:
                    try:
                        next(ga)
                    except StopIteration:
                        a_done = True

        def run_pass(pi):
            sample = (pi == 0)
            v = 0 if sample else 1
            L = LS if sample else LP
            nseq = 1 if sample else 2
            Tm = nseq * L
            NT = Tm // 128
            NCH = Tm // 512
            N = L // 128
            Tw = WIN if sample else 512
            tag = "s" if sample else "p"
            if sample:
                tiles_w = [(0, 128), (128, 128), (256, 2)]
            else:
                tiles_w = [(i * 128, 128) for i in range(4)]
            prow0 = (pi - 1) * 512
            with ExitStack() as pp:
                gsm = sb([128, 16], F32, pp); gsf = sb([128, 16], F32, pp)
                gvm = sb([128, 16], F32, pp); gvf = sb([128, 16], F32, pp)
                VB = Buf("vecs")
                S.op("dve", lambda e: e.scalar_tensor_tensor(gsm[:], modT[:, 16:32, v], 1.0, gnorm[:, 0, :], ALU.add, ALU.mult), r=[CB], w=[VB])
                VB2 = Buf("vecs2")

                def late_vecs():
                    S.op("dve", lambda e: e.scalar_tensor_tensor(gsf[:], modT[:, 64:80, v], 1.0, gnorm[:, 1, :], ALU.add, ALU.mult), r=[CB, MODB], w=[VB2])
                    S.op("dve", lambda e: e.tensor_tensor(gvm[:], modT[:, 32:48, v], gnorm[:, 2, :], ALU.mult), r=[CB, MODB], w=[VB2])
                    S.op("dve", lambda e: e.tensor_tensor(gvf[:], modT[:, 80:96, v], gnorm[:, 3, :], ALU.mult), r=[CB, MODB], w=[VB2])
                shm = lambda kc: modT[:, kc, v:v + 1]
                shf = lambda kc: modT[:, 48 + kc, v:v + 1]

                pm = ExitStack()
                pr = ExitStack()
                pp.callback(pr.close)
                pp.callback(pm.close)
                hT = sb([128, 16, Tm], BF16, pm)
                HB = Buf("hT")
                yhyTw = sb([128, 8, Tw], BF16, pm); yretTw = sb([128, 8, Tw], BF16, pm)
                YHB = Buf("yhyTw"); YRB = Buf("yretTw")

                with ExitStack() as ph:
                    if sample:
                        rows = [(din["xs_full"][i * 128:(i + 1) * 128, :], 128, None) for i in range(NT)]
                    else:
                        rows = [(din["xp"][prow0 + i * 128:prow0 + (i + 1) * 128, :], 128, None) for i in range(NT)]
                    norm_T(ph, rows, hT, HB, gsm, shm, VB)
                    S.barrier()
                dump(f"hT{pi}", hT[:, :, 0:128], [HB])
                if stop_after == "norm1":
                    return

                with ExitStack() as ph:
                    if sample:
                        ropeC = sb([128, LS], F32, ph); ropeS = sb([128, LS], F32, ph)
                        sel = sb([128, 8, WIN], BF16, ph)
                        st0 = sb([128, 16, 128], F32, ph)
                        S.join("pool")
                        S.dma("sp", ropeC[:], din["ropeC"], w=[CB])
                        S.dma("sp", ropeS[:], din["ropeS"], w=[CB])
                        S.dma("pool", sel[:], din["sel"].rearrange("p (t w) -> p t w", w=WIN), w=[CB])
                        S.dma("sp", st0[:], din["state0"].rearrange("d h p e -> p (d h) e"), w=[CB])
                        QTf = sb([128, Tm], F32, ph); KTf = sb([128, Tm], F32, ph)
                        t1 = sb([128, 512], F32, ph); t2 = sb([128, 512], F32, ph)
                    QFB, KFB, T1B, T2B = Buf("QTf"), Buf("KTf"), Buf("t1"), Buf("t2")

                    def mk_ws():
                        w_ = {}
                        w_["QT"] = sb([128, Tm], BF16, ph); w_["KT"] = sb([128, Tm], BF16, ph)
                        w_["Qxf"] = sb([128, Tm], BF16, ph); w_["Qxb"] = sb([128, Tm], BF16, ph)
                        w_["Vh"] = sb([128, NT, 128], BF16, ph); w_["Gh"] = sb([128, NT, 128], BF16, ph)
                        w_["Kzf"] = sb([128, NT, 128], BF16, ph); w_["Kzb"] = sb([128, NT, 128], BF16, ph)
                        w_["Rf"] = [sb([128, 128], F32, ph) for _ in range(2)]
                        w_["Rb"] = [sb([128, 128], F32, ph) for _ in range(2)]
                        w_["Rfb"] = sb([128, NT, 128], BF16, ph); w_["Rbb"] = sb([128, NT, 128], BF16, ph)
                        w_["Sm"] = sb([128, NT, 128], BF16, ph)
                        w_["on"] = sb([128, NT, 128], F32, ph)
                        w_["st6"] = sb([128, NT, 6], F32, ph); w_["mv"] = sb([128, NT, 2], F32, ph); w_["rsd"] = sb([128, NT], F32, ph)
                        w_["yr"] = sb([128, NT, 128], BF16, ph)
                        for nm in ("QB", "KB_", "QXB", "VB_", "GB_", "KZB", "RFbB", "RBbB", "ONB", "STB", "YRtB"):
                            w_[nm] = Buf(nm)
                        w_["RFB"] = [Buf("Rf0"), Buf("Rf1")]; w_["RBB"] = [Buf("Rb0"), Buf("Rb1")]
                        w_["SMB"] = Buf("SmA")
                        return w_

                    WS = [mk_ws(), mk_ws()]
                    stg = [sb([128, 128], F32, ph) for _ in range(6)]
                    STGB = [Buf(f"stg{i}") for i in range(6)]
                    stg_i = [0]
                    def ret_A(h):
                        w_ = WS[h % 2]
                        QT, KT, Qxf, Qxb, Vh, Gh, Kzf, Kzb = (w_[k] for k in ("QT", "KT", "Qxf", "Qxb", "Vh", "Gh", "Kzf", "Kzb"))
                        Rf, Rb, Rfb, Rbb, SmA, on, st6, mv, rsd, yr = (w_[k] for k in ("Rf", "Rb", "Rfb", "Rbb", "Sm", "on", "st6", "mv", "rsd", "yr"))
                        QB, KB_, QXB, VB_, GB_, KZB, RFbB, RBbB, ONB, STB, YRtB = (w_[k] for k in ("QB", "KB_", "QXB", "VB_", "GB_", "KZB", "RFbB", "RBbB", "ONB", "STB", "YRtB"))
                        RFB, RBB, SMB = w_["RFB"], w_["RBB"], w_["SMB"]
                        wq, wqb = w_take("wqk", h)
                        wqv = wq[:, 0:4096].rearrange("p (k n) -> p k n", n=256)
                        for which in range(2):
                            for c in range(NCH):
                                bank = (which * NCH + c) % 2
                                for kc in range(16):
                                    S.op("pe", lambda e: e.matmul(pbank[bank][:, :], wqv[:, kc, which * 128:(which + 1) * 128],
                                                                  hT[:, kc, c * 512:(c + 1) * 512], start=(kc == 0), stop=(kc == 15)),
                                         r=[wqb, HB], w=PB(bank), sig=(kc == 15))
                                cs = slice(c * 512, (c + 1) * 512)
                                scl = 1.0 if which == 0 else 128.0 ** -0.5
                                if not sample:
                                    dst, dB = (QT, QB) if which == 0 else (KT, KB_)
                                    S.op("act", lambda e: e.activation(dst[:, cs], pbank[bank][:, :], AF.Copy, scale=scl), r=PB(bank), w=[dB])
                                    yield
                                else:
                                    dstf, dfB = (QTf, QFB) if which == 0 else (KTf, KFB)
                                    dst, dB = (QT, QB) if which == 0 else (KT, KB_)
                                    S.op("act", lambda e: e.activation(dstf[:, cs], pbank[bank][:, :], AF.Copy, scale=scl), r=PB(bank), w=[dfB])
                                    S.op("pe", lambda e: e.matmul(pbank[2][:, :], pswap[:], dstf[:, cs], start=True, stop=True),
                                         r=[CB, dfB], w=PB(2))
                                    S.op("dve", lambda e: e.tensor_tensor(t1[:], dstf[:, cs], ropeC[:, cs], ALU.mult), r=[dfB, CB], w=[T1B])
                                    S.op("dve", lambda e: e.tensor_tensor(t2[:], pbank[2][:, :], ropeS[:, cs], ALU.mult), r=PB(2) + [CB], w=[T2B])
                                    S.op("dve", lambda e: e.tensor_tensor(dst[:, cs], t1[:], t2[:], ALU.add), r=[T1B, T2B], w=[dB])
                                    yield
                        wv_, wvb = w_take("wvg", h)
                        wvv = wv_[:, 0:4096].rearrange("p (k n) -> p k n", n=256)
                        for g_ in range(NT):
                            bank = (g_ % 2) if sample else (3 if g_ % 2 == 0 else 2)
                            q = 0
                            for kc in range(16):
                                S.op("pe", lambda e: e.matmul(pbank[bank][:, q * 256:(q + 1) * 256], hT[:, kc, g_ * 128:(g_ + 1) * 128],
                                                              wvv[:, kc, :], start=(kc == 0), stop=(kc == 15)),
                                     r=[wvb, HB], w=PB(bank, q * 256, (q + 1) * 256), sig=(kc == 15))
                            S.op("act", lambda e: e.activation(Vh[:, g_, :], pbank[bank][:, q * 256:q * 256 + 128], AF.Copy),
                                 r=PB(bank, q * 256, q * 256 + 128), w=[VB_])
                            S.op("act", lambda e: e.activation(Gh[:, g_, :], pbank[bank][:, q * 256 + 128:(q + 1) * 256], AF.Silu),
                                 r=PB(bank, q * 256 + 128, (q + 1) * 256), w=[GB_])
                            yield
                        if h == 0:
                            dump(f"QT{pi}", QT[:, 0:128], [QB]); dump(f"KT{pi}", KT[:, 0:128], [KB_])
                            dump(f"Vh{pi}", Vh[:, 0, :], [VB_]); dump(f"Gh{pi}", Gh[:, 0, :], [GB_])
                        QT3 = QT[:].rearrange("p (g i) -> p g i", i=128)
                        S.op("dve", lambda e: e.tensor_tensor(Qxf[:].rearrange("p (g i) -> p g i", i=128), QT3,
                                                              xi_bc[:, h, :].unsqueeze(1).to_broadcast([128, NT, 128]), ALU.mult), r=[QB, CB], w=[QXB])
                        S.op("dve", lambda e: e.tensor_tensor(Qxb[:].rearrange("p (g i) -> p g i", i=128), QT3,
                                                              xi_bc[:, 8 + h, :].unsqueeze(1).to_broadcast([128, NT, 128]), ALU.mult), r=[QB, CB], w=[QXB])
                        yield
                        for g_ in range(NT):
                            S.op("pe", lambda e: e.transpose(pbf(6)[:, g_ * 128:(g_ + 1) * 128], KT[:, g_ * 128:(g_ + 1) * 128], ident_b[:]),
                                 r=[KB_, CB], w=PB(6), sig=(g_ == NT - 1))
                        S.op("act", lambda e: e.activation(Kzf[:].rearrange("p g d -> p (g d)"), pbf(6)[:, 0:NT * 128], AF.Copy, scale=zcol[:, h:h + 1]),
                             r=PB(6) + [CB], w=[KZB])
                        S.op("act", lambda e: e.activation(Kzb[:].rearrange("p g d -> p (g d)"), pbf(6)[:, 0:NT * 128], AF.Copy, scale=zcol[:, 8 + h:9 + h]),
                             r=PB(6) + [CB], w=[KZB])
                        yield
                        yield
                    def ret_B(h):
                        w_ = WS[h % 2]
                        QT, KT, Qxf, Qxb, Vh, Gh, Kzf, Kzb = (w_[k] for k in ("QT", "KT", "Qxf", "Qxb", "Vh", "Gh", "Kzf", "Kzb"))
                        Rf, Rb, Rfb, Rbb, SmA, on, st6, mv, rsd, yr = (w_[k] for k in ("Rf", "Rb", "Rfb", "Rbb", "Sm", "on", "st6", "mv", "rsd", "yr"))
                        QB, KB_, QXB, VB_, GB_, KZB, RFbB, RBbB, ONB, STB, YRtB = (w_[k] for k in ("QB", "KB_", "QXB", "VB_", "GB_", "KZB", "RFbB", "RBbB", "ONB", "STB", "YRtB"))
                        RFB, RBB, SMB = w_["RFB"], w_["RBB"], w_["SMB"]
                        for d_ in range(2):
                            Kz = Kzf if d_ == 0 else Kzb
                            R, RB_ = (Rf, RFB) if d_ == 0 else (Rb, RBB)
                            Rbf, RbfB = (Rfb, RFbB) if d_ == 0 else (Rbb, RBbB)
                            cd = cdcol[:, d_ * 8 + h:d_ * 8 + h + 1]
                            for g_ in range(NT):
                                bank = 4 + (g_ // 4) % 2
                                q = g_ % 4
                                S.op("pe", lambda e: e.matmul(pbank[bank][:, q * 128:(q + 1) * 128], Kz[:, g_, :], Vh[:, g_, :], start=True, stop=True),
                                     r=[KZB, VB_], w=PB(bank, q * 128, (q + 1) * 128))
                            for s_ in range(nseq):
                                cur = 0
                                if sample:
                                    S.op("dve", lambda e: e.tensor_copy(R[0][:], st0[:, d_ * 8 + h, :]), r=[CB], w=[RB_[0]])
                                else:
                                    S.op("dve", lambda e: e.memset(R[0][:], 0.0), w=[RB_[0]])
                                order = list(range(N)) if d_ == 0 else list(range(N - 1, -1, -1))
                                for c in order:
                                    g_ = s_ * N + c
                                    bank = 4 + (g_ // 4) % 2
                                    q = g_ % 4
                                    S.op("act", lambda e: e.activation(Rbf[:, g_, :], R[cur][:], AF.Copy), r=[RB_[cur]], w=[RbfB])
                                    if c == order[-1] and not sample:
                                        k_ = stg_i[0] % len(stg); stg_i[0] += 1
                                        rdst, rdB = stg[k_], STGB[k_]
                                    else:
                                        rdst, rdB = R[1 - cur], RB_[1 - cur]
                                    S.op("dve", lambda e: e.scalar_tensor_tensor(rdst[:], R[cur][:], cd, pbank[bank][:, q * 128:(q + 1) * 128],
                                                                                 ALU.mult, ALU.add),
                                         r=[RB_[cur], CB] + PB(bank, q * 128, (q + 1) * 128), w=[rdB])
                                    yield
                                    cur = 1 - cur
                                if not sample:
                                    S.dma("sp", stout[(pi - 1) * 2 + s_, d_, h], rdst[:], r=[rdB], w=[Buf()], is_out=True)
                        sbks = (3, 5) if not sample else (3, 3)
                        for g_ in range(NT):
                            sbk = sbks[g_ % 2]
                            S.op("pe", lambda e: e.matmul(pbank[sbk][:, 0:128], KT[:, g_ * 128:(g_ + 1) * 128], QT[:, g_ * 128:(g_ + 1) * 128],
                                                          start=True, stop=True), r=[KB_, QB], w=PB(sbk))
                            S.op("dve", lambda e: e.tensor_tensor(SmA[:, g_, :], pbank[sbk][:, 0:128], msum[:, h, :], ALU.mult),
                                 r=PB(sbk) + [CB], w=[SMB])
                            yield
                        for g_ in range(NT):
                            ob = 7 if g_ < 4 else 4
                            oc = slice((g_ % 4) * 128, (g_ % 4 + 1) * 128)
                            S.op("pe", lambda e: e.matmul(pbank[ob][:, oc], SmA[:, g_, :], Vh[:, g_, :], start=True, stop=False),
                                 r=[SMB, VB_], w=PB(ob), sig=False)
                            S.op("pe", lambda e: e.matmul(pbank[ob][:, oc], Qxf[:, g_ * 128:(g_ + 1) * 128], Rfb[:, g_, :], start=False, stop=False),
                                 r=[QXB, RFbB], w=PB(ob), sig=False)
                            S.op("pe", lambda e: e.matmul(pbank[ob][:, oc], Qxb[:, g_ * 128:(g_ + 1) * 128], Rbb[:, g_, :], start=False, stop=True),
                                 r=[QXB, RBbB], w=PB(ob))
                            if g_ % 4 == 3:
                                yield
                        if h == 0:
                            dump(f"oraw{pi}", pbank[7][:, 0:128], PB(7))
                        for g_ in range(NT):
                            ob = 7 if g_ < 4 else 4
                            oc = slice((g_ % 4) * 128, (g_ % 4 + 1) * 128)
                            S.op("dve", lambda e: e.bn_stats(st6[:, g_, :], pbank[ob][:, oc]), r=PB(ob), w=[STB])
                            S.op("dve", lambda e: e.bn_aggr(mv[:, g_, :], st6[:, g_, :]), r=[STB], w=[STB])
                            if g_ % 2 == 1:
                                yield
                        rstd_from_ss(mv[:, :, 1], rsd[:, :], [STB], [STB], epsg[:], 1.0)
                        yield
                        for g_ in range(NT):
                            ob = 7 if g_ < 4 else 4
                            oc = slice((g_ % 4) * 128, (g_ % 4 + 1) * 128)
                            S.op("dve", lambda e: e.tensor_scalar(on[:, g_, :], pbank[ob][:, oc], mv[:, g_, 0:1], rsd[:, g_:g_ + 1], ALU.subtract, ALU.mult),
                                 r=PB(ob) + [STB], w=[ONB])
                            if g_ % 2 == 1:
                                yield
                        S.op("dve", lambda e: e.tensor_tensor(on[:], on[:], gn_bc[:, h * 128:(h + 1) * 128].unsqueeze(1).to_broadcast([128, NT, 128]), ALU.mult),
                             r=[ONB, CB], w=[ONB])
                        S.op("dve", lambda e: e.tensor_tensor(yr[:], on[:], Gh[:], ALU.mult), r=[ONB, GB_], w=[YRtB])
                        yield
                        if h == 0:
                            dump(f"yr{pi}", yr[:, 0, :], [YRtB])
                        if sample:
                            for g_ in range(NT):
                                S.op("pe", lambda e: e.matmul(pbank[2][:, 0:WIN], yr[:, g_, :], sel[:, g_, :], start=(g_ == 0), stop=(g_ == NT - 1)),
                                     r=[YRtB, CB], w=PB(2), sig=(g_ == NT - 1))
                            S.op("act", lambda e: e.activation(yretTw[:, h, :], pbank[2][:, 0:WIN], AF.Copy), r=PB(2), w=[YRB])
                        else:
                            for g_ in range(NT):
                                S.op("pe", lambda e: e.transpose(pbf(6)[:, g_ * 128:(g_ + 1) * 128], yr[:, g_, :], ident_b[:]),
                                     r=[YRtB, CB], w=PB(6), sig=(g_ == NT - 1))
                            S.op("act", lambda e: e.activation(yretTw[:, h, :], pbf(6)[:, 0:NT * 128], AF.Copy), r=PB(6), w=[YRB])
                        yield


                    mrowL = [sb([2, 512], F32, ph) for _ in range(2)]
                    MBL = [Buf("mrowL0"), Buf("mrowL1")]
                    drive(ret_A(0), None)
                    for h in range(8):
                        drive(ret_A(h + 1) if h + 1 < 8 else None, ret_B(h))
                        if late_mod[0] < 24:
                            for _ in range(2):
                                cb_ = late_mod[0]
                                mod_cblk(cb_, mrowL[cb_ % 2], MBL[cb_ % 2], 2, 6, MODB)
                                late_mod[0] += 1
                    S.barrier()
                dump(f"yretTw{pi}", yretTw[:, 0, 0:128], [YRB])
                if stop_after == "ret":
                    return

                with ExitStack() as ph:
                    nt = N
                    Cm = sb([128, nt, L], BF16, ph); Sn = sb([128, nt, L], BF16, ph)
                    nyqc = sb([128, nt], BF16, ph); nyqr = sb([1, L], BF16, ph)
                    wfc = sb([128, nt], F32, ph); negt = sb([128, nt], F32, ph)
                    DB = Buf("dftc")
                    S.join("pool")
                    S.dma("pool", Cm[:], din[f"dft{tag}_C"].rearrange("p (t f) -> p t f", f=L), w=[DB])
                    S.dma("pool", Sn[:], din[f"dft{tag}_S"].rearrange("p (t f) -> p t f", f=L), w=[DB])
                    S.dma("pool", nyqc[:], din[f"dft{tag}_nyq_col"], w=[DB])
                    S.dma("pool", nyqr[:], din[f"dft{tag}_nyq_row"], w=[DB])
                    S.dma("sp", wfc[:], din[f"dft{tag}_wf"], w=[DB])
                    S.dma("sp", negt[:], din[f"dft{tag}_negt"], w=[DB])
                    if sample:
                        sel = sb([128, 8, WIN], BF16, ph)
                        S.dma("pool", sel[:], din["sel"].rearrange("p (t w) -> p t w", w=WIN), w=[DB])
                    h2T = sb([65, L], F32, ph)
                    H2B = Buf("h2T")
                    with ExitStack() as ph2:
                        zT = sb([33, L], F32, ph2); w1 = sb([33, 64], F32, ph2); w2 = sb([64, 64], F32, ph2)
                        b12 = sb([64, 2], F32, ph2); fr = sb([64, 2], F32, ph2); fb = sb([64, 2], F32, ph2)
                        h1T = sb([64, L], F32, ph2); arg = sb([64, 512], F32, ph2); kk = sb([64, 512], F32, ph2)
                        MLB = Buf("mlp"); AB = Buf("arg"); H1B = Buf("h1T")
                        S.dma("sp", zT[:], din[f"dft{tag}_zT"], w=[MLB])
                        S.dma("sp", w1[:], din["hyw1"], w=[MLB])
                        S.dma("sp", w2[:], din["hyw2"], w=[MLB])
                        S.dma("sp", b12[:], din["hyb12"], w=[MLB])
                        S.dma("sp", fr[:], din["hyfreq"], w=[MLB])
                        S.op("dve", lambda e: e.tensor_tensor(fb[:], fr[:], b12[:], ALU.mult), r=[MLB], w=[MLB])
                        S.op("dve", lambda e: e.memset(h2T[64:65, :], 1.0), w=[H2B])
                        for layer in range(2):
                            for c in range((L + 511) // 512):
                                n_ = min(512, L - c * 512)
                                cs = slice(c * 512, c * 512 + n_)
                                if layer == 0:
                                    S.op("pe", lambda e: e.matmul(pbank[0][0:64, 0:n_], w1[:], zT[:, cs], start=True, stop=True), r=[MLB], w=PB(0))
                                else:
                                    S.op("pe", lambda e: e.matmul(pbank[0][0:64, 0:n_], w2[:], h1T[:, cs], start=True, stop=True), r=[MLB, H1B], w=PB(0))
                                S.op("dve", lambda e: e.tensor_scalar(arg[:, 0:n_], pbank[0][0:64, 0:n_], fr[:, layer:layer + 1], fb[:, layer:layer + 1],
                                                                      ALU.mult, ALU.add), r=PB(0) + [MLB], w=[AB])
                                S.op("dve", lambda e: e.tensor_scalar(kk[:, 0:n_], arg[:, 0:n_], 1.0 / TWO_PI, MAGIC, ALU.mult, ALU.add), r=[AB], w=[AB])
                                S.op("dve", lambda e: e.tensor_scalar(kk[:, 0:n_], kk[:, 0:n_], -MAGIC, None, ALU.add), r=[AB], w=[AB])
                                S.op("dve", lambda e: e.scalar_tensor_tensor(arg[:, 0:n_], kk[:, 0:n_], -TWO_PI, arg[:, 0:n_], ALU.mult, ALU.add), r=[AB], w=[AB])
                                S.op("dve", lambda e: e.tensor_scalar(arg[:, 0:n_], arg[:, 0:n_], math.pi, -math.pi, ALU.min, ALU.max), r=[AB], w=[AB])
                                if layer == 0:
                                    S.op("act", lambda e: e.activation(h1T[:, cs], arg[:, 0:n_], AF.Sin), r=[AB], w=[H1B])
                                else:
                                    S.op("act", lambda e: e.activation(h2T[0:64, cs], arg[:, 0:n_], AF.Sin), r=[AB], w=[H2B])
                        S.barrier()
                    dump(f"h2T{pi}", h2T[0:64, 0:128], [H2B])

                    def mk_hws():
                        w_ = {}
                        w_["w3s"] = sb([65, 512], F32, ph); w_["adec"] = sb([128, 512], F32, ph)
                        w_["Et"] = sb([128, 512], F32, ph); w_["hts"] = sb([128, 512], F32, ph); w_["habs"] = sb([128, 512], BF16, ph)
                        w_["hsum"] = sb([128, 512], F32, ph)
                        w_["s_bf"] = sb([128, nt, 256], BF16, ph); w_["d_bf"] = sb([128, nt, 256], BF16, ph)
                        w_["rn"] = sb([128, 512], F32, ph); w_["tmpn"] = w_["Et"][:, 0:256]
                        w_["Ksp"] = sb([128, nt, 512], BF16, ph); w_["Knyq"] = sb([1, 256], F32, ph)
                        w_["raw"] = sb([128, Tm], F32, ph)
                        w_["x1c"] = sb([128, Tm], BF16, ph); w_["x2c"] = sb([128, Tm], BF16, ph); w_["uu"] = sb([128, Tm], F32, ph)
                        w_["u_bf"] = sb([128, Tm], BF16, ph)
                        w_["zz"] = sb([128, Tm], F32, ph)
                        w_["u_tm"] = sb([128, NT, 128], BF16, ph)
                        w_["ta"] = sb([128, 512], F32, ph); w_["tb_"] = sb([128, 512], F32, ph)
                        w_["Yre"] = sb([128, nt, 128], BF16, ph); w_["Yim"] = sb([128, nt, 128], BF16, ph); w_["Ynq"] = sb([1, 128], BF16, ph)
                        w_["yT"] = w_["u_bf"]
                        for nm in ("W3B", "ADB", "ETB", "HTB", "HAB", "SDB", "RNB", "KSB", "RAWB", "X1B", "X2B", "UB", "UBB", "UTB", "TAB", "TBB", "YB", "YTB", "ZB"):
                            w_[nm] = Buf(nm)
                        return w_

                    if sample:
                        h0_ = mk_hws()
                        h1_ = dict(h0_)
                        h1_["Ksp"] = sb([128, nt, 512], BF16, ph); h1_["Knyq"] = sb([1, 256], F32, ph)
                        h1_["x1c"] = sb([128, Tm], BF16, ph); h1_["x2c"] = sb([128, Tm], BF16, ph); h1_["uu"] = sb([128, Tm], F32, ph)
                        for nm in ("KSB", "X1B", "X2B", "UB"):
                            h1_[nm] = Buf(nm + "b")
                        HWS = [h0_, h1_]
                    else:
                        HWS = [mk_hws(), mk_hws()]
                    def hy_A(cb):
                        w_ = HWS[cb % len(HWS)]
                        w3s, adec, Et, hts, habs, s_bf, d_bf, rn, tmpn, Ksp, Knyq, raw = (w_[k] for k in ("w3s", "adec", "Et", "hts", "habs", "s_bf", "d_bf", "rn", "tmpn", "Ksp", "Knyq", "raw"))
                        x1c, x2c, uu, u_bf, u_tm, ta, tb_, Yre, Yim, Ynq, yT = (w_[k] for k in ("x1c", "x2c", "uu", "u_bf", "u_tm", "ta", "tb_", "Yre", "Yim", "Ynq", "yT"))
                        zz = w_["zz"]
                        hsum = w_["hsum"]
                        W3B, ADB, ETB, HTB, HAB, SDB, RNB, KSB, RAWB = (w_[k] for k in ("W3B", "ADB", "ETB", "HTB", "HAB", "SDB", "RNB", "KSB", "RAWB"))
                        X1B, X2B, UB, UBB, UTB, TAB, TBB, YB, YTB = (w_[k] for k in ("X1B", "X2B", "UB", "UBB", "UTB", "TAB", "TBB", "YB", "YTB"))
                        ZB = w_["ZB"]; YTB = UBB
                        S.dma("sp", w3s[:], din["w3b"][cb], w=[W3B])
                        S.dma("sp", adec[:], din["decb"][cb:cb + 1, :].partition_broadcast(128), w=[ADB])
                        S.op("dve", lambda e: e.scalar_tensor_tensor(adec[:], adec[:], -1.0, adec[:], ALU.mult, ALU.max), r=[ADB], w=[ADB])
                        for tt in range(nt):
                            S.op("pe", lambda e: e.matmul(pbank[7][:, :], h2T[:, tt * 128:(tt + 1) * 128], w3s[:], start=True, stop=True),
                                 r=[H2B, W3B], w=PB(7))
                            S.op("act", lambda e: e.activation(Et[:], adec[:], AF.Exp, scale=negt[:, tt:tt + 1]), r=[ADB, DB], w=[ETB])
                            S.op("dve", lambda e: e.tensor_tensor(hts[:], pbank[7][:, :], Et[:], ALU.mult), r=PB(7) + [ETB], w=[HTB])
                            if tt == 0:
                                S.op("dve", lambda e: e.memset(hts[0:1, 256:512], 0.0), w=[HTB])
                                S.op("act", lambda e: e.activation(hsum[:], hts[:], AF.Abs), r=[HTB], w=[HAB])
                            else:
                                S.op("act", lambda e: e.activation(habs[:], hts[:], AF.Abs), r=[HTB], w=[HAB])
                                S.op("dve", lambda e: e.tensor_tensor(hsum[:], hsum[:], habs[:], ALU.add), r=[HAB], w=[HAB])
                            S.op("dve", lambda e: e.tensor_tensor(s_bf[:, tt, :], hts[:, 0:256], hts[:, 256:512], ALU.add), r=[HTB], w=[SDB])
                            S.op("dve", lambda e: e.tensor_tensor(d_bf[:, tt, :], hts[:, 0:256], hts[:, 256:512], ALU.subtract), r=[HTB], w=[SDB])
                            yield
                        S.op("dve", lambda e: e.tensor_tensor(tmpn, hsum[:, 0:256], hsum[:, 256:512], ALU.add), r=[HAB], w=[RNB, ETB])
                        S.op("pe", lambda e: e.matmul(pbank[7][:, 0:256], ones_f[:], tmpn, start=True, stop=True), r=[CB, RNB, ETB], w=PB(7))
                        S.op("dve", lambda e: e.tensor_scalar(rn[:, 0:256], pbank[7][:, 0:256], FILTER_EPS, None, ALU.add), r=PB(7), w=[RNB])
                        S.op("dve", lambda e: e.reciprocal(rn[:, 0:256], rn[:, 0:256]), r=[RNB], w=[RNB])
                        S.op("dve", lambda e: e.tensor_copy(rn[:, 256:512], rn[:, 0:256]), r=[RNB], w=[RNB])
                        for ft in range(nt):
                            bank = 7 - ft % 2
                            for tt in range(nt):
                                S.op("pe", lambda e: e.matmul(pbank[bank][:, 0:256], Cm[:, tt, ft * 128:(ft + 1) * 128], s_bf[:, tt, :],
                                                              start=(tt == 0), stop=(tt == nt - 1)), r=[DB, SDB], w=PB(bank, 0, 256), sig=(tt == nt - 1))
                            for tt in range(nt):
                                S.op("pe", lambda e: e.matmul(pbank[bank][:, 256:512], Sn[:, tt, ft * 128:(ft + 1) * 128], d_bf[:, tt, :],
                                                              start=(tt == 0), stop=(tt == nt - 1)), r=[DB, SDB], w=PB(bank, 256, 512), sig=(tt == nt - 1))
                            S.op("dve", lambda e: e.scalar_tensor_tensor(Ksp[:, ft, :], pbank[bank][:, :], wfc[:, ft:ft + 1], rn[:], ALU.mult, ALU.mult),
                                 r=PB(bank) + [DB, RNB], w=[KSB])
                            yield
                        for tt in range(nt):
                            S.op("pe", lambda e: e.matmul(pbank[7][0:1, 0:256], nyqc[:, tt:tt + 1], s_bf[:, tt, :], start=(tt == 0), stop=(tt == nt - 1)),
                                 r=[DB, SDB], w=PB(7), sig=(tt == nt - 1))
                        S.op("dve", lambda e: e.scalar_tensor_tensor(Knyq[:], pbank[7][0:1, 0:256], 1.0 / (2 * L), rn[0:1, 0:256], ALU.mult, ALU.mult),
                             r=PB(7) + [RNB], w=[KSB])
                        if cb == 0:
                            dump(f"Ksp{pi}", Ksp[:, 0, :], [KSB])

                        wtA, wbA = w_take("whyA", cb)
                        wvA = wtA[:, 0:4096].rearrange("p (k n) -> p k n", n=256)
                        for part in range(3):
                            if part == 2:
                                wtB, wbB = w_take("whyB", cb)
                                wvB = wtB[:, 0:2048].rearrange("p (k n) -> p k n", n=128)
                            wvp, wb = (wvA, wbA) if part < 2 else (wvB, wbB)
                            pc0 = part * 128 if part < 2 else 0
                            for c in range(NCH):
                                bank = (part * NCH + c) % 2
                                for kc in range(16):
                                    S.op("pe", lambda e: e.matmul(pbank[bank][:, :], wvp[:, kc, pc0:pc0 + 128], hT[:, kc, c * 512:(c + 1) * 512],
                                                                  start=(kc == 0), stop=(kc == 15)), r=[wb, HB], w=PB(bank), sig=(kc == 15))
                                S.op("act", lambda e: e.activation(raw[:, c * 512:(c + 1) * 512], pbank[bank][:, :], AF.Copy), r=PB(bank), w=[RAWB])
                                yield
                            conv3(uu[:], raw[:], hsw[:, part * 8 + cb, :], nseq, L, [RAWB], [UB])
                            yield
                            if part < 2:
                                dst, dB = [(x1c, X1B), (x2c, X2B)][part]
                                S.op("act", lambda e: e.activation(dst[:], uu[:], AF.Copy), r=[UB], w=[dB])
                        if cb == 0:
                            dump(f"uu{pi}", uu[:, 0:128], [UB])
                        yield
                    def hy_B(cb):
                        w_ = HWS[cb % len(HWS)]
                        w3s, adec, Et, hts, habs, s_bf, d_bf, rn, tmpn, Ksp, Knyq, raw = (w_[k] for k in ("w3s", "adec", "Et", "hts", "habs", "s_bf", "d_bf", "rn", "tmpn", "Ksp", "Knyq", "raw"))
                        x1c, x2c, uu, u_bf, u_tm, ta, tb_, Yre, Yim, Ynq, yT = (w_[k] for k in ("x1c", "x2c", "uu", "u_bf", "u_tm", "ta", "tb_", "Yre", "Yim", "Ynq", "yT"))
                        zz = w_["zz"]
                        hsum = w_["hsum"]
                        W3B, ADB, ETB, HTB, HAB, SDB, RNB, KSB, RAWB = (w_[k] for k in ("W3B", "ADB", "ETB", "HTB", "HAB", "SDB", "RNB", "KSB", "RAWB"))
                        X1B, X2B, UB, UBB, UTB, TAB, TBB, YB, YTB = (w_[k] for k in ("X1B", "X2B", "UB", "UBB", "UTB", "TAB", "TBB", "YB", "YTB"))
                        ZB = w_["ZB"]; YTB = UBB
                        for o_ in range(2):
                            src, sB = (uu, UB) if o_ == 0 else (zz, ZB)
                            gate, gB = (x1c, X1B) if o_ == 0 else (x2c, X2B)
                            S.op("act", lambda e: e.activation(u_bf[:], src[:], AF.Copy), r=[sB], w=[UBB])
                            for g_ in range(NT):
                                S.op("pe", lambda e: e.transpose(pbf(6 + (g_ // 8) % 2)[:, (g_ % 8) * 128:(g_ % 8 + 1) * 128], u_bf[:, g_ * 128:(g_ + 1) * 128], ident_b[:]),
                                     r=[UBB, CB], w=PB(6), sig=(g_ == NT - 1))
                            S.op("act", lambda e: e.activation(u_tm[:].rearrange("p g c -> p (g c)"), pbf(6)[:, 0:NT * 128], AF.Copy), r=PB(6), w=[UTB])
                            yield
                            for s_ in range(nseq):
                                for fg in range((nt + 3) // 4):
                                    nf_ = min(4, nt - fg * 4)
                                    for fi in range(nf_):
                                        ft = fg * 4 + fi
                                        for tt in range(nt):
                                            S.op("pe", lambda e: e.matmul(pbank[2][:, fi * 128:(fi + 1) * 128], Cm[:, tt, ft * 128:(ft + 1) * 128], u_tm[:, s_ * nt + tt, :],
                                                                          start=(tt == 0), stop=(tt == nt - 1)), r=[DB, UTB], w=PB(2, fi * 128, (fi + 1) * 128), sig=(tt == nt - 1))
                                        for tt in range(nt):
                                            S.op("pe", lambda e: e.matmul(pbank[3][:, fi * 128:(fi + 1) * 128], Sn[:, tt, ft * 128:(ft + 1) * 128], u_tm[:, s_ * nt + tt, :],
                                                                          start=(tt == 0), stop=(tt == nt - 1)), r=[DB, UTB], w=PB(3, fi * 128, (fi + 1) * 128), sig=(tt == nt - 1))
                                    w_ = nf_ * 128
                                    fs = slice(fg * 4, fg * 4 + nf_)
                                    Ure = pbank[2][:, 0:w_].rearrange("p (f c) -> p f c", c=128)
                                    Uim = pbank[3][:, 0:w_].rearrange("p (f c) -> p f c", c=128)
                                    Kre = Ksp[:, fs, o_ * 128:(o_ + 1) * 128]
                                    Kim = Ksp[:, fs, 256 + o_ * 128:256 + (o_ + 1) * 128]
                                    ta3 = ta[:, 0:w_].rearrange("p (f c) -> p f c", c=128)
                                    tb3 = tb_[:, 0:w_].rearrange("p (f c) -> p f c", c=128)
                                    S.op("dve", lambda e: e.tensor_tensor(ta3, Ure, Kre, ALU.mult), r=PB(2, 0, w_) + [KSB], w=[TAB])
                                    S.op("dve", lambda e: e.tensor_tensor(tb3, Uim, Kim, ALU.mult), r=PB(3, 0, w_) + [KSB], w=[TBB])
                                    S.op("dve", lambda e: e.tensor_tensor(Yre[:, fs, :], ta3, tb3, ALU.subtract), r=[TAB, TBB], w=[YB])
                                    yield
                                    S.op("dve", lambda e: e.tensor_tensor(ta3, Ure, Kim, ALU.mult), r=PB(2, 0, w_) + [KSB], w=[TAB])
                                    S.op("dve", lambda e: e.tensor_tensor(tb3, Uim, Kre, ALU.mult), r=PB(3, 0, w_) + [KSB], w=[TBB])
                                    S.op("dve", lambda e: e.tensor_tensor(Yim[:, fs, :], ta3, tb3, ALU.add), r=[TAB, TBB], w=[YB])
                                    yield
                                for tt in range(nt):
                                    S.op("pe", lambda e: e.matmul(pbank[2][0:1, 0:128], nyqc[:, tt:tt + 1], u_tm[:, s_ * nt + tt, :], start=(tt == 0), stop=(tt == nt - 1)),
                                         r=[DB, UTB], w=PB(2, 0, 128), sig=(tt == nt - 1))
                                S.op("dve", lambda e: e.tensor_tensor(Ynq[:], pbank[2][0:1, 0:128], Knyq[0:1, o_ * 128:(o_ + 1) * 128], ALU.mult),
                                     r=PB(2, 0, 128) + [KSB], w=[YB])
                                yield
                                for c in range((L + 511) // 512):
                                    n_ = min(512, L - c * 512)
                                    bank = 4 + (c + s_) % 2
                                    cs = slice(c * 512, c * 512 + n_)
                                    for ft in range(nt):
                                        S.op("pe", lambda e: e.matmul(pbank[bank][:, 0:n_], Yre[:, ft, :], Cm[:, ft, cs], start=(ft == 0), stop=False),
                                             r=[YB, DB], w=PB(bank), sig=False)
                                        S.op("pe", lambda e: e.matmul(pbank[bank][:, 0:n_], Yim[:, ft, :], Sn[:, ft, cs], start=False, stop=False),
                                             r=[YB, DB], w=PB(bank), sig=False)
                                    S.op("pe", lambda e: e.matmul(pbank[bank][:, 0:n_], Ynq[:], nyqr[0:1, cs], start=False, stop=True), r=[YB, DB], w=PB(bank))
                                    gs_ = slice(s_ * L + c * 512, s_ * L + c * 512 + n_)
                                    S.op("dve", lambda e: e.scalar_tensor_tensor(ta[:, 0:n_], src[:, gs_], hbias[:, o_, cb:cb + 1], pbank[bank][:, 0:n_], ALU.mult, ALU.add),
                                         r=[sB, CB] + PB(bank), w=[TAB])
                                    if o_ == 0:
                                        S.op("dve", lambda e: e.tensor_tensor(zz[:, gs_], ta[:, 0:n_], gate[:, gs_], ALU.mult), r=[TAB, gB], w=[ZB])
                                        yield
                                    else:
                                        S.op("dve", lambda e: e.tensor_tensor(yT[:, gs_], ta[:, 0:n_], gate[:, gs_], ALU.mult), r=[TAB, gB], w=[YTB])
                                        yield
                        if cb == 0:
                            dump(f"yT{pi}", yT[:, 0:128], [YTB])
                        if sample:
                            for g_ in range(NT):
                                S.op("pe", lambda e: e.transpose(pbf(6)[:, g_ * 128:(g_ + 1) * 128], yT[:, g_ * 128:(g_ + 1) * 128], ident_b[:]),
                                     r=[YTB, CB], w=PB(6), sig=(g_ == NT - 1))
                            S.op("act", lambda e: e.activation(u_tm[:].rearrange("p g c -> p (g c)"), pbf(6)[:, 0:NT * 128], AF.Copy), r=PB(6), w=[UTB])
                            yield
                            for g_ in range(NT):
                                S.op("pe", lambda e: e.matmul(pbank[2][:, 0:WIN], u_tm[:, g_, :], sel[:, g_, :], start=(g_ == 0), stop=(g_ == NT - 1)),
                                     r=[UTB, DB], w=PB(2), sig=(g_ == NT - 1))
                            S.op("act", lambda e: e.activation(yhyTw[:, cb, :], pbank[2][:, 0:WIN], AF.Copy), r=PB(2), w=[YHB])
                        else:
                            S.op("act", lambda e: e.activation(yhyTw[:, cb, :], yT[:], AF.Copy), r=[YTB], w=[YHB])
                        yield

                    if len(HWS) == 1:
                        for cb in range(8):
                            drive(hy_A(cb), None)
                            drive(None, hy_B(cb))
                    else:
                        drive(hy_A(0), None)
                        for cb in range(8):
                            drive(hy_A(cb + 1) if cb + 1 < 8 else None, hy_B(cb))
                    S.barrier()
                dump(f"yhyTw{pi}", yhyTw[:, 0, 0:128], [YHB])
                if stop_after == "hy":
                    return

                xmid = sb([128, len(tiles_w), D], F32, pr, side="right")
                XMB = [Buf(f"xmid{i}") for i in range(len(tiles_w))]
                h2 = sb([128, 16, Tw], BF16, pr, side="right")
                H2TB = Buf("h2")

                def x_rows_w():
                    if sample:
                        return [(din["xs_win"][r0:r0 + rr, :], rr, None) for (r0, rr) in tiles_w]
                    return [(din["xp"][prow0 + r0:prow0 + r0 + rr, :], rr, None) for (r0, rr) in tiles_w]

                def bc_rows(ph, gv, dst, dstB):
                    dg = sb([128, 128], F32, ph)
                    DGB = Buf("dg")
                    for kc in range(16):
                        S.op("dve", lambda e: e.tensor_scalar(dg[:], ident_f[:], gv[:, kc:kc + 1], None, ALU.mult), r=[CB, VB2], w=[DGB])
                        bank = 4 + (kc // 4) % 2
                        q = kc % 4
                        S.op("pe", lambda e: e.matmul(pbank[bank][:, q * 128:(q + 1) * 128], ones_f[:], dg[:], start=True, stop=True),
                             r=[CB, DGB], w=PB(bank, q * 128, (q + 1) * 128))
                        if q == 3:
                            S.op("act", lambda e: e.activation(dst[:, (kc - 3) * 128:(kc + 1) * 128], pbank[bank][:, :], AF.Copy), r=PB(bank), w=[dstB])

                late_vecs()
                with ExitStack() as ph:
                    if sample:
                        hTw = sb([128, 16, Tw], BF16, ph)
                        HWB = Buf("hTw")
                        with ExitStack() as ph2:
                            norm_T(ph2, x_rows_w(), hTw, HWB, gsm, shm, VB)
                            S.barrier()
                    else:
                        hTw, HWB = hT, HB
                    gm_bc = sb([128, D], F32, ph)
                    GMB = Buf("gm_bc")
                    bc_rows(ph, gvm, gm_bc, GMB)
                    mergedT = sb([128, 16, Tw], BF16, ph)
                    MGB = Buf("mergedT")
                    sg = [sb([128, Tw], F32, ph) for _ in range(4)]
                    tm = [sb([128, Tw], F32, ph) for _ in range(2)]
                    SGB = [Buf(f"sg{i}") for i in range(4)]
                    TMB = [Buf("tm0"), Buf("tm1")]
                    for n in range(16):
                        wt, wb = w_take("wmg", n)
                        wg = wt[:, 0:4096].rearrange("p (k n) -> p k n", n=256)
                        pb0 = (n % 2) * 4
                        for i_ in range(2):
                            sgi = (n % 2) * 2 + i_
                            for kc in range(16):
                                S.op("pe", lambda e: e.matmul(pbank[pb0 + i_][:, 0:Tw], wg[:, kc, i_ * 128:(i_ + 1) * 128], hTw[:, kc, :], start=(kc == 0), stop=(kc == 15)),
                                     r=[wb, HWB], w=PB(pb0 + i_), sig=(kc == 15))
                            S.op("act", lambda e: e.activation(sg[sgi][:], pbank[pb0 + i_][:, 0:Tw], AF.Sigmoid), r=PB(pb0 + i_), w=[SGB[sgi]])
                        wt2, wb2 = w_take("wmb", n)
                        wbh = wt2[:, 0:1024].rearrange("p (k n) -> p k n", n=128)
                        wbr = wt2[:, 1024:2048].rearrange("p (k n) -> p k n", n=128)
                        for i_, (wbx, ysrc, yB) in enumerate([(wbh, yhyTw, YHB), (wbr, yretTw, YRB)]):
                            sgi = (n % 2) * 2 + i_
                            for cc in range(8):
                                S.op("pe", lambda e: e.matmul(pbank[pb0 + 2 + i_][:, 0:Tw], wbx[:, cc, :], ysrc[:, cc, :], start=(cc == 0), stop=(cc == 7)),
                                     r=[wb2, yB], w=PB(pb0 + 2 + i_), sig=(cc == 7))
                            S.op("dve", lambda e: e.tensor_tensor(tm[i_][:], sg[sgi][:], pbank[pb0 + 2 + i_][:, 0:Tw], ALU.mult), r=[SGB[sgi]] + PB(pb0 + 2 + i_), w=[TMB[i_]])
                        S.op("dve", lambda e: e.tensor_tensor(mergedT[:, n, :], tm[0][:], tm[1][:], ALU.add), r=TMB, w=[MGB])
                    dump(f"mergedT{pi}", mergedT[:, 0, 0:128], [MGB])
                    junk = sb([128, 512], BF16, ph)
                    ssq = sb([128, len(tiles_w), 4], F32, ph)
                    rsm = sb([128, len(tiles_w)], F32, ph)
                    JB = Buf("junk"); SQB = Buf("ssq")
                    S.op("dve", lambda e: e.memset(ssq[:], 0.0), w=[SQB])
                    for cc in range(4):
                        base = (cc % 2) * 4
                        for half in range(2):
                            w0, wb0 = w_take("wout", cc * 2 + half)
                            wv0 = w0[:, 0:4096].rearrange("p (k n) -> p k n", n=512)
                            for ti, (r0, rr) in enumerate(tiles_w):
                                for k in range(8):
                                    kc = half * 8 + k
                                    S.op("pe", lambda e: e.matmul(pbank[base + ti][0:rr, :], mergedT[:, kc, r0:r0 + rr], wv0[:, k, :], start=(kc == 0), stop=(kc == 15)),
                                         r=[MGB, wb0], w=PB(base + ti), sig=(k == 7))
                        for ti, (r0, rr) in enumerate(tiles_w):
                            bank = base + ti
                            S.op("act", lambda e: e.activation(xmid[0:rr, ti, cc * 512:(cc + 1) * 512], pbank[bank][0:rr, :], AF.Copy), r=PB(bank), w=[XMB[ti]])
                            S.op("act", lambda e: e.activation(junk[0:rr, :], pbank[bank][0:rr, :], AF.Square, accum_out=ssq[0:rr, ti, cc:cc + 1]),
                                 r=PB(bank), w=[JB, SQB])
                    xin = [sb([128, D], F32, ph)]
                    XIB = [Buf("xi0")]
                    rows2 = []
                    for ti, (r0, rr) in enumerate(tiles_w):
                        b = 0
                        S.op("dve", lambda e: e.tensor_reduce(rsm[0:rr, ti:ti + 1], ssq[0:rr, ti, :], mybir.AxisListType.X, ALU.add), r=[SQB], w=[SQB])
                        rstd_from_ss(rsm[0:rr, ti:ti + 1], rsm[0:rr, ti:ti + 1], [SQB], [SQB], epsr[0:rr, :], 1.0 / D)
                        src = x_rows_w()[ti][0]
                        S.dma("sp", xin[b][0:rr, :], src, w=[XIB[b]])
                        S.op("dve", lambda e: e.scalar_tensor_tensor(xmid[0:rr, ti, :], xmid[0:rr, ti, :], rsm[0:rr, ti:ti + 1], gm_bc[0:rr, :], ALU.mult, ALU.mult),
                             r=[XMB[ti], SQB, GMB], w=[XMB[ti]])
                        S.op("dve", lambda e: e.tensor_tensor(xmid[0:rr, ti, :], xmid[0:rr, ti, :], xin[b][0:rr, :], ALU.add), r=[XMB[ti], XIB[b]], w=[XMB[ti]])
                        rows2.append((xmid[:, ti, :], rr, XMB[ti]))
                    dump(f"xmid{pi}", xmid[:, 0, 0:128], [XMB[0]])
                    with ExitStack() as ph2:
                        norm_T(ph2, rows2, h2, H2TB, gsf, shf, VB2)
                    if sample:
                        for wc, mc in ((0, 0), (WIN - 1, 1)):
                            S.op("dve", lambda e: e.tensor_scalar(h2[:, :, wc:wc + 1], h2[:, :, wc:wc + 1], hmask[:, mc:mc + 1], None, ALU.mult), r=[H2TB, CB], w=[H2TB])
                    S.barrier()
                pm.close()
                if stop_after == "mid":
                    return

                with ExitStack() as ph:
                    gf_bc = sb([128, D], F32, ph)
                    GFB = Buf("gf_bc")
                    bc_rows(ph, gvf, gf_bc, GFB)
                    actT = sb([128, NJ, Tw], BF16, ph)
                    ACB = Buf("actT")
                    p4a = ExitStack()
                    ph.callback(p4a.close)
                    ac = [sb([128, Tw], F32, p4a) for _ in range(2)]
                    bc_ = [sb([128, Tw], F32, p4a) for _ in range(2)]
                    gl = [sb([128, Tw], F32, p4a) for _ in range(2)]
                    ACB_ = [Buf("ac0"), Buf("ac1")]; BCB = [Buf("bc0"), Buf("bc1")]; GLB = [Buf("gl0"), Buf("gl1")]
                    seglen = WIN if sample else LP
                    nseg = Tw // seglen
                    for j in range(NJ):
                        wt, wb = w_take("wup", j)
                        wu = wt[:, 0:4096].rearrange("p (k n) -> p k n", n=256)
                        p_ = j % 2
                        for i_ in range(2):
                            bank = p_ * 2 + i_
                            for kc in range(16):
                                S.op("pe", lambda e: e.matmul(pbank[bank][:, 0:Tw], wu[:, kc, i_ * 128:(i_ + 1) * 128], h2[:, kc, :], start=(kc == 0), stop=(kc == 15)),
                                     r=[wb, H2TB], w=PB(bank), sig=(kc == 15))
                        conv3(ac[p_][:], pbank[p_ * 2][:, 0:Tw], fcw[:, j, :], nseg, seglen, PB(p_ * 2), [ACB_[p_]])
                        conv3(bc_[p_][:], pbank[p_ * 2 + 1][:, 0:Tw], fcw[:, NJ + j, :], nseg, seglen, PB(p_ * 2 + 1), [BCB[p_]])
                        S.op("act", lambda e: e.activation(gl[p_][:], ac[p_][:], AF.Gelu_apprx_tanh), r=[ACB_[p_]], w=[GLB[p_]])
                        S.op("dve", lambda e: e.tensor_tensor(actT[:, j, :], gl[p_][:], bc_[p_][:], ALU.mult), r=[GLB[p_], BCB[p_]], w=[ACB])
                    dump(f"actT{pi}", actT[:, 0, 0:128], [ACB])
                    S.barrier()
                    p4a.close()
                    fsb = sb([128, len(tiles_w), D], F32, ph)
                    FB = [Buf(f"f{i}") for i in range(len(tiles_w))]
                    junk = sb([128, 512], BF16, ph)
                    ssq = sb([128, len(tiles_w), 4], F32, ph)
                    rsm = sb([128, len(tiles_w)], F32, ph)
                    JB = Buf("junk"); SQB = Buf("ssq")
                    S.op("dve", lambda e: e.memset(ssq[:], 0.0), w=[SQB])
                    for cc in range(4):
                        base = (cc % 2) * 4
                        for jg in range(6):
                            njg = 8 if jg < 5 else 4
                            wt, wb = w_take("wdn", cc * 6 + jg)
                            wd = wt[:, 0:njg * 512].rearrange("p (j n) -> p j n", n=512)
                            for ti, (r0, rr) in enumerate(tiles_w):
                                for jj in range(njg):
                                    j = jg * 8 + jj
                                    last = (j == NJ - 1)
                                    S.op("pe", lambda e: e.matmul(pbank[base + ti][0:rr, :], actT[:, j, r0:r0 + rr], wd[:, jj, :], start=(j == 0), stop=last),
                                         r=[ACB, wb], w=PB(base + ti), sig=(jj == njg - 1))
                        for ti, (r0, rr) in enumerate(tiles_w):
                            S.op("act", lambda e: e.activation(fsb[0:rr, ti, cc * 512:(cc + 1) * 512], pbank[base + ti][0:rr, :], AF.Copy), r=PB(base + ti), w=[FB[ti]])
                            S.op("act", lambda e: e.activation(junk[0:rr, :], pbank[base + ti][0:rr, :], AF.Square, accum_out=ssq[0:rr, ti, cc:cc + 1]),
                                 r=PB(base + ti), w=[JB, SQB])
                    for ti, (r0, rr) in enumerate(tiles_w):
                        S.op("dve", lambda e: e.tensor_reduce(rsm[0:rr, ti:ti + 1], ssq[0:rr, ti, :], mybir.AxisListType.X, ALU.add), r=[SQB], w=[SQB])
                        rstd_from_ss(rsm[0:rr, ti:ti + 1], rsm[0:rr, ti:ti + 1], [SQB], [SQB], epsr[0:rr, :], 1.0 / D)
                        S.op("dve", lambda e: e.scalar_tensor_tensor(fsb[0:rr, ti, :], fsb[0:rr, ti, :], rsm[0:rr, ti:ti + 1], gf_bc[0:rr, :], ALU.mult, ALU.mult),
                             r=[FB[ti], SQB, GFB], w=[FB[ti]])
                        S.op("dve", lambda e: e.tensor_tensor(fsb[0:rr, ti, :], fsb[0:rr, ti, :], xmid[0:rr, ti, :], ALU.add), r=[FB[ti], XMB[ti]], w=[FB[ti]])
                        if sample:
                            if ti == 0:
                                S.dma("sp", ys[0:127, :], fsb[1:128, ti, :], r=[FB[ti]], w=[Buf()], is_out=True)
                            elif ti == 1:
                                S.dma("sp", ys[127:255, :], fsb[0:128, ti, :], r=[FB[ti]], w=[Buf()], is_out=True)
                            else:
                                S.dma("sp", ys[255:256, :], fsb[0:1, ti, :], r=[FB[ti]], w=[Buf()], is_out=True)
                        else:
                            S.dma("sp", yp[prow0 + r0:prow0 + r0 + rr, :], fsb[0:rr, ti, :], r=[FB[ti]], w=[Buf()], is_out=True)
                    S.barrier()

        for pi in passes:
            run_pass(pi)
            if stop_after is not None:
                break
        S.barrier()
        S.finish()
    return nc


_CACHE = {}


def kernel(**inputs):
    sh, per = host_prep(inputs)
    nc = build()
    in_maps = []
    for core in range(8):
        m = dict(sh)
        m.update(per[core])
        in_maps.append(m)
    res = run_bass_kernel_spmd(nc, in_maps, core_ids=list(range(8)))
    B, L_, D_ = inputs["x_prompt"].shape
    y_prompt = np.zeros((32, LP, D), np.float32)
    y_sample = np.zeros((2, LS, D), np.float32)
    new_state = np.zeros((32, 1, 2, 8, 128, 128), np.float32)
    for core in range(8):
        r = res.results[core]
        y_prompt[4 * core:4 * core + 4] = np.asarray(r["yp"]).reshape(4, LP, D)
        s, j = core // 4, core % 4
        y_sample[s, 256 * j:256 * (j + 1)] = np.asarray(r["ys"])
        new_state[4 * core:4 * core + 4, 0] = np.asarray(r["stout"])
    return (y_prompt, y_sample, new_state)
```

```python
import math
from contextlib import ExitStack

import numpy as np
import concourse.bass as bass
import concourse.mybir as mybir
from concourse.bass_utils import run_bass_kernel_spmd

F32 = mybir.dt.float32
BF16 = mybir.dt.bfloat16
AF = mybir.ActivationFunctionType
ALU = mybir.AluOpType

D = 2048
KC = 16
DFF = 5632
NJ = 44
LP = 256
LS = 1024
WIN = 258
SLOT = 4096
NSLOT = 3
RMS_EPS = 1e-6
GN_EPS = 1e-5
FILTER_EPS = 1e-6
MAGIC = 12582912.0
TWO_PI = 2.0 * math.pi


class Buf:
    __slots__ = ("name", "w", "r")

    def __init__(self, name=""):
        self.name = name
        self.w = None
        self.r = {}


class Sched:
    ROLL = 16000
    NDSEM = 12

    def __init__(self, nc, es):
        self.nc = nc
        self.es = es
        self.eng = {"pe": nc.tensor, "dve": nc.vector, "act": nc.scalar, "pool": nc.gpsimd, "sp": nc.sync}
        self.sem = {}
        self.cnt = {}
        self.seen = {k: {} for k in self.eng}
        self.owner = {}
        self.pend = {k: [] for k in self.eng}
        self.nsem = 0
        self.keep = []
        for k in self.eng:
            self._newsem(k)
        self.dq = {q: {"sems": [], "uses": [], "next": 0} for q in ("sp", "act", "pool")}
        self.out_toks = []

    def _mk(self, name):
        self.nsem += 1
        s = self.es.enter_context(self.nc.semaphore(f"{name}_{self.nsem}"))
        self.keep.append(s)
        return s

    def _newsem(self, k):
        self.sem[k] = self._mk("e" + k)
        self.owner[id(self.sem[k])] = k
        self.cnt[k] = 0

    def _wait(self, e, tok):
        sem, val = tok
        key = id(sem)
        if e == "pe" and self.owner.get(key) == "pe":
            return
        if self.seen[e].get(key, 0) >= val:
            return
        self.seen[e][key] = val
        self.eng[e].wait_ge(sem, val)

    def _deps(self, e, reads, writes):
        for b in reads:
            if b.w is not None:
                self._wait(e, b.w)
        for b in writes:
            if b.w is not None:
                self._wait(e, b.w)
            for t in list(b.r.values()):
                self._wait(e, t)

    def _commit(self, tok, reads, writes):
        key = id(tok[0])
        for b in reads:
            b.r[key] = tok
        for b in writes:
            b.w = tok
            b.r = {}

    def op(self, e, fn, r=(), w=(), sig=True):
        for oe, p in self.pend.items():
            assert oe == e or not p, f"pending unsignaled ops on {oe} while emitting on {e}"
        self._deps(e, r, w)
        inst = fn(self.eng[e])
        if not sig:
            self.pend[e].append((r, w))
            return None
        if self.cnt[e] >= self.ROLL:
            self._newsem(e)
        self.cnt[e] += 1
        inst.then_inc(self.sem[e], 1)
        tok = (self.sem[e], self.cnt[e])
        for (pr, pw) in self.pend[e]:
            self._commit(tok, pr, pw)
        self.pend[e] = []
        self._commit(tok, r, w)
        return tok

    def dma(self, q, out, in_, r=(), w=(), is_out=False, **kw):
        for oe, p in self.pend.items():
            assert not p, f"pending unsignaled ops on {oe} while emitting dma on {q}"
        self._deps(q, r, w)
        d = self.dq[q]
        if len(d["sems"]) < self.NDSEM:
            d["sems"].append(self._mk("d" + q))
            d["uses"].append(0)
            i = len(d["sems"]) - 1
        else:
            i = d["next"] % self.NDSEM
        d["next"] += 1
        if d["uses"][i] >= 900:
            d["sems"][i] = self._mk("d" + q)
            d["uses"][i] = 0
        sem = d["sems"][i]
        if d["uses"][i] > 0:
            self._wait(q, (sem, 16 * d["uses"][i]))
        d["uses"][i] += 1
        self.eng[q].dma_start(out=out, in_=in_, **kw).then_inc(sem, 16)
        tok = (sem, 16 * d["uses"][i])
        self._commit(tok, r, w)
        if is_out:
            self.out_toks.append(tok)
        return tok

    def barrier(self, engines=("pe", "dve", "act", "sp")):
        toks = []
        for e in engines:
            assert not self.pend[e]
            if self.cnt[e] > 0:
                toks.append((self.sem[e], self.cnt[e]))
        for q in ("sp", "act"):
            d = self.dq[q]
            for s, u in zip(d["sems"], d["uses"]):
                if u > 0:
                    toks.append((s, 16 * u))
        self.last_barrier = toks
        for e in engines:
            for t in toks:
                if self.owner.get(id(t[0])) == e:
                    continue
                self._wait(e, t)

    def join(self, e):
        for t in getattr(self, "last_barrier", []):
            self._wait(e, t)

    def finish(self):
        for t in self.out_toks:
            self._wait("sp", t)


def _ktile(w):
    n = w.shape[1]
    return np.ascontiguousarray(w.reshape(KC, 128, n).transpose(1, 0, 2)).reshape(128, KC * n)


def _fm(v, nchunk):
    return np.ascontiguousarray(v.reshape(nchunk, 128).T)


def _dft_consts(L):
    n = np.arange(L, dtype=np.float64)
    ang = np.pi * np.outer(n, n) / L
    C = np.cos(ang).astype(np.float32)
    S = (-np.sin(ang)).astype(np.float32)
    nt = L // 128

    def tile(m):
        return np.ascontiguousarray(m.reshape(nt, 128, L).transpose(1, 0, 2)).reshape(128, nt * L)

    nyq = ((-1.0) ** n).astype(np.float32)
    nyq_col = np.ascontiguousarray(nyq.reshape(nt, 128).T)
    nyq_row = nyq.reshape(1, L)
    wf = np.full(L, 1.0 / L, np.float32)
    wf[0] = 1.0 / (2 * L)
    wf_col = np.ascontiguousarray(wf.reshape(nt, 128).T)
    negt = np.ascontiguousarray((-(n / L)).astype(np.float32).reshape(nt, 128).T)
    nf = np.arange(L, dtype=np.float32)
    t = nf / np.float32(L)
    f = np.linspace(1e-4, 16 - 1, 16, dtype=np.float32)
    w = (np.float32(2.0 * math.pi) * nf / np.float32(L)).astype(np.float32)
    z = np.concatenate([t[:, None], np.cos(w[:, None] * f), np.sin(w[:, None] * f)], axis=-1).astype(np.float32)
    zT = np.ascontiguousarray(z.T)
    return dict(C=tile(C), S=tile(S), nyq_col=nyq_col, nyq_row=nyq_row, wf=wf_col, negt=negt, zT=zT)


def _rope_consts():
    L = LS
    rows = L // 64
    row = np.repeat(np.arange(rows), 64).astype(np.float32)
    col = np.tile(np.arange(64), rows).astype(np.float32)
    inv = (np.float32(10000.0) ** (-np.arange(32, dtype=np.float32) / np.float32(32))).astype(np.float32)
    ang = np.concatenate([row[:, None] * inv, col[:, None] * inv], axis=-1)
    c = np.cos(ang).astype(np.float32).T
    s = np.sin(ang).astype(np.float32).T
    CC = np.concatenate([c, c], axis=0)
    SS = np.concatenate([-s, s], axis=0)
    return np.ascontiguousarray(CC), np.ascontiguousarray(SS)


def _ret_consts():
    i = np.arange(128, dtype=np.float32)
    J, I = np.meshgrid(i, i, indexing="ij")
    dpos = np.maximum(I - J, 0.0)
    dneg = np.maximum(J - I, 0.0)
    mge = (I >= J).astype(np.float32)
    mle = (J >= I).astype(np.float32)
    iota1 = np.broadcast_to(i[None, :] + 1.0, (128, 128))
    cmi = np.broadcast_to(128.0 - i[None, :], (128, 128))
    rc = np.stack([dpos, dneg, mge, mle, iota1, cmi], axis=1).astype(np.float32)
    pc = np.stack([127.0 - i, i, np.full(128, 128.0, np.float32)], axis=1).astype(np.float32)
    return np.ascontiguousarray(rc), np.ascontiguousarray(pc)


def host_prep(inp):
    g = lambda k: np.asarray(inp[k], dtype=np.float32)
    w_in = g("w_in")[0]
    sh = {}
    sh["wada"] = np.stack([_ktile(g("w_ada")[0][:, b * 512:(b + 1) * 512]).reshape(128, 2, 8 * 512)[:, h]
                           for b in range(24) for h in range(2)])
    sh["whyA"] = np.stack([_ktile(np.concatenate([w_in[:, cb * 128:(cb + 1) * 128],
                                                  w_in[:, 1024 + cb * 128:1024 + (cb + 1) * 128]], axis=1))
                           for cb in range(8)])
    sh["whyB"] = np.stack([_ktile(w_in[:, 2048 + cb * 128:2048 + (cb + 1) * 128]) for cb in range(8)])
    o = 3072
    sh["wqk"] = np.stack([_ktile(np.concatenate([w_in[:, o + h * 128:o + (h + 1) * 128],
                                                 w_in[:, o + 1024 + h * 128:o + 1024 + (h + 1) * 128]], axis=1))
                          for h in range(8)])
    sh["wvg"] = np.stack([_ktile(np.concatenate([w_in[:, o + 2048 + h * 128:o + 2048 + (h + 1) * 128],
                                                 w_in[:, o + 3072 + h * 128:o + 3072 + (h + 1) * 128]], axis=1))
                          for h in range(8)])
    og = 7168
    wbh = g("w_br_hy")[0]
    wbr = g("w_br_ret")[0]

    def brt(w, n):
        return np.ascontiguousarray(w[:, n * 128:(n + 1) * 128].reshape(8, 128, 128).transpose(1, 0, 2)).reshape(128, 1024)

    sh["wmg"] = np.stack([_ktile(np.concatenate([w_in[:, og + n * 128:og + (n + 1) * 128],
                                                 w_in[:, og + 2048 + n * 128:og + 2048 + (n + 1) * 128]], axis=1))
                          for n in range(16)])
    sh["wmb"] = np.stack([np.concatenate([brt(wbh, n), brt(wbr, n)], axis=1) for n in range(16)])
    wo = g("w_out")[0]
    sh["wout"] = np.stack([_ktile(wo[:, cc * 512:(cc + 1) * 512]).reshape(128, 2, 8 * 512)[:, h]
                           for cc in range(4) for h in range(2)])
    wu = g("ffn_w_up")[0]
    sh["wup"] = np.stack([_ktile(np.concatenate([wu[:, j * 128:(j + 1) * 128],
                                                 wu[:, DFF + j * 128:DFF + (j + 1) * 128]], axis=1))
                          for j in range(NJ)])
    wd = g("ffn_w_down")[0]
    wdn = np.zeros((24, 128, 4096), np.float32)
    for cc in range(4):
        for jg in range(6):
            nj = 8 if jg < 5 else 4
            blk = wd[jg * 8 * 128:(jg * 8 + nj) * 128, cc * 512:(cc + 1) * 512].reshape(nj, 128, 512).transpose(1, 0, 2)
            wdn[cc * 6 + jg, :, 0:nj * 512] = blk.reshape(128, nj * 512)
    sh["wdn"] = wdn
    sh = {k: np.ascontiguousarray(v, dtype=np.float32) for k, v in sh.items()}
    sh["badaT"] = _fm(g("b_ada")[0], 96)
    sh["gnorm"] = np.ascontiguousarray(np.stack([_fm(g("norm_pre_mix")[0], 16), _fm(g("norm_pre_ffn")[0], 16),
                                                 _fm(g("norm_post_mix")[0], 16), _fm(g("norm_post_ffn")[0], 16)], axis=1))
    hs = g("hy_short_w")[0]
    sh["hsw"] = np.ascontiguousarray(hs.reshape(3, 24, 128).transpose(2, 1, 0))
    sh["hbias"] = np.ascontiguousarray(g("hy_bias")[0].reshape(2, 8, 128).transpose(2, 0, 1))
    fc = g("ffn_conv")[0]
    sh["fcw"] = np.ascontiguousarray(fc.reshape(3, 88, 128).transpose(2, 1, 0))
    sh["retgn"] = g("ret_gn")[0].reshape(1, 1024)
    sh["retlogit"] = g("ret_decay_logit")[0].reshape(1, 16)
    sh["hyw1"] = g("hy_w1")[0]
    sh["hyw2"] = g("hy_w2")[0]
    sh["hyb12"] = np.ascontiguousarray(np.stack([g("hy_b1")[0], g("hy_b2")[0]], axis=1))
    sh["hyfreq"] = np.ascontiguousarray(g("hy_freq")[0].T)
    w3 = g("hy_w3")[0]
    b3 = g("hy_b3")[0]
    dec = g("hy_decay")[0].reshape(4096)
    w3b = np.zeros((8, 65, 512), np.float32)
    decb = np.zeros((8, 512), np.float32)
    for cb in range(8):
        cols = np.concatenate([np.arange(128) + d_ * 2048 + o_ * 1024 + cb * 128 for d_ in range(2) for o_ in range(2)])
        w3b[cb, :64] = w3[:, cols]
        w3b[cb, 64] = b3[cols]
        decb[cb] = dec[cols]
    sh["w3b"] = w3b
    sh["decb"] = decb
    sh["ident"] = np.eye(128, dtype=np.float32)
    ps = np.zeros((128, 128), np.float32)
    for dp in range(128):
        ps[(dp + 64) % 128, dp] = 1.0
    sh["pswap"] = ps
    for L, tag in ((LP, "p"), (LS, "s")):
        dc = _dft_consts(L)
        for k, v in dc.items():
            sh[f"dft{tag}_{k}"] = v
    sh["ropeC"], sh["ropeS"] = _rope_consts()
    sh["retc"], sh["retpc"] = _ret_consts()

    xs = g("x_sample")
    xp = g("x_prompt")
    st = g("state_ret")
    c = g("c")
    cctx = g("c_ctx")
    per = []
    for core in range(8):
        s, j = core // 4, core % 4
        m = {}
        m["xs_full"] = xs[s]
        win = np.zeros((WIN, D), np.float32)
        sel = np.zeros((LS, WIN), np.float32)
        for w_ in range(WIN):
            t = 256 * j - 1 + w_
            if 0 <= t < LS:
                win[w_] = xs[s, t]
                sel[t, w_] = 1.0
        m["xs_win"] = win
        m["sel"] = np.ascontiguousarray(sel.reshape(8, 128, WIN).transpose(1, 0, 2)).reshape(128, 8 * WIN)
        m["hmask"] = np.ascontiguousarray(np.broadcast_to(
            np.array([[1.0 if j > 0 else 0.0, 1.0 if j < 3 else 0.0]], np.float32), (128, 2)))
        m["xp"] = np.ascontiguousarray(xp[4 * core:4 * core + 4].reshape(4 * LP, D))
        m["cT"] = np.ascontiguousarray(np.stack([_fm(c[s], 16), _fm(cctx, 16)], axis=2))
        m["state0"] = np.ascontiguousarray(st[s, 0])
        per.append(m)
    return sh, per


SHAPES_SHARED = {
    "wada": [48, 128, 4096], "whyA": [8, 128, 4096], "whyB": [8, 128, 2048], "wqk": [8, 128, 4096], "wvg": [8, 128, 4096],
    "wmg": [16, 128, 4096], "wmb": [16, 128, 2048], "wout": [8, 128, 4096], "wup": [44, 128, 4096], "wdn": [24, 128, 4096],
    "badaT": [128, 96], "gnorm": [128, 4, 16], "hsw": [128, 24, 3], "hbias": [128, 2, 8], "fcw": [128, 88, 3],
    "retgn": [1, 1024], "retlogit": [1, 16], "hyw1": [33, 64], "hyw2": [64, 64], "hyb12": [64, 2],
    "hyfreq": [64, 2], "w3b": [8, 65, 512], "decb": [8, 512], "ident": [128, 128], "pswap": [128, 128],
    "ropeC": [128, LS], "ropeS": [128, LS], "retc": [128, 6, 128], "retpc": [128, 3],
}
for _L, _tag in ((LP, "p"), (LS, "s")):
    _nt = _L // 128
    SHAPES_SHARED.update({f"dft{_tag}_C": [128, _nt * _L], f"dft{_tag}_S": [128, _nt * _L],
                          f"dft{_tag}_nyq_col": [128, _nt], f"dft{_tag}_nyq_row": [1, _L],
                          f"dft{_tag}_wf": [128, _nt], f"dft{_tag}_negt": [128, _nt], f"dft{_tag}_zT": [33, _L]})
SHAPES_CORE = {"xs_full": [LS, D], "xs_win": [WIN, D], "sel": [128, 8 * WIN], "hmask": [128, 2],
               "xp": [4 * LP, D], "cT": [128, 16, 2], "state0": [2, 8, 128, 128]}


def build(cfg=None):
    cfg = cfg or {}
    passes = cfg.get("passes", [0, 1, 2])
    dumps = cfg.get("dump", {})
    stop_after = cfg.get("stop_after", None)
    rstop = cfg.get("ret_stop", 99)
    nc = bass.Bass("TRN2", target_bir_lowering=False)
    din = {}
    for k, shp in list(SHAPES_SHARED.items()) + list(SHAPES_CORE.items()):
        din[k] = nc.dram_tensor(k, list(shp), F32, kind="ExternalInput").ap()
    yp = nc.dram_tensor("yp", [4 * LP, D], F32, kind="ExternalOutput").ap()
    ys = nc.dram_tensor("ys", [256, D], F32, kind="ExternalOutput").ap()
    stout = nc.dram_tensor("stout", [4, 2, 8, 128, 128], F32, kind="ExternalOutput").ap()
    dbg_out = {k: nc.dram_tensor("dbg_" + k, list(shp), F32, kind="ExternalOutput").ap() for k, shp in dumps.items()}

    with ExitStack() as es:
        S = Sched(nc, es)
        cnt = [0]

        def sb(shape, dt, scope=None, name=None, side=None):
            cnt[0] += 1
            kw = {"side": side} if side else {}
            return (scope or es).enter_context(nc.sbuf_tensor(f"{name or 't'}{cnt[0]}", list(shape), dt, **kw))

        pbank = [es.enter_context(nc.psum_tensor(f"pb{i}", [128, 512], F32)) for i in range(8)]
        pbq = [[Buf(f"pb{i}q{q}") for q in range(4)] for i in range(8)]

        def PB(i, c0=0, c1=512):
            return [pbq[i][0]]

        def pbf(i):
            return pbank[i][:].bitcast(BF16)

        def PBb(i, c0=0, c1=1024):
            return [pbq[i][0]]

        ring = [sb([128, SLOT], BF16, name="ring") for _ in range(NSLOT)]
        ringb = [Buf(f"ring{i}") for i in range(NSLOT)]
        wplan = []

        def plan_pass(first):
            for h in range(8):
                wplan.append(("wqk", h, 4096))
                wplan.append(("wvg", h, 4096))
                if first and h >= 1:
                    for i in range(4):
                        wplan.append(("wada", 16 + (h - 1) * 4 + i, 4096))
            if first:
                for i in range(4):
                    wplan.append(("wada", 16 + 7 * 4 + i, 4096))
            for cb in range(8):
                wplan.append(("whyA", cb, 4096))
                wplan.append(("whyB", cb, 2048))
            for n in range(16):
                wplan.append(("wmg", n, 4096))
                wplan.append(("wmb", n, 2048))
            for i in range(8):
                wplan.append(("wout", i, 4096))
            for j in range(NJ):
                wplan.append(("wup", j, 4096))
            for i in range(24):
                wplan.append(("wdn", i, 4096 if i % 6 < 5 else 2048))

        for i in range(16):
            wplan.append(("wada", i, 4096))
        for pidx, _ in enumerate(passes):
            plan_pass(pidx == 0)
        wstate = {"issued": 0, "taken": 0}

        def w_issue():
            i = wstate["issued"]
            if i >= len(wplan):
                return
            name, idx, n = wplan[i]
            S.dma("pool", ring[i % NSLOT][:, 0:n], din[name][idx][:, 0:n], w=[ringb[i % NSLOT]])
            wstate["issued"] += 1

        def w_take(name, idx, prev_done=True):
            i = wstate["taken"]
            assert wplan[i][0] == name and wplan[i][1] == idx, (wplan[i], name, idx)
            while wstate["issued"] < min(len(wplan), i + NSLOT - (0 if prev_done else 1)):
                w_issue()
            wstate["taken"] += 1
            return ring[i % NSLOT], ringb[i % NSLOT]

        dbg_tmp = {k: sb(list(shp), F32, name="dbg", side="right") for k, shp in dumps.items()}

        def dump(name, ap, bufs):
            if name in dbg_out:
                tmp = dbg_tmp[name]
                tb = Buf("dbg" + name)
                S.op("dve", lambda e: e.tensor_copy(tmp[:], ap), r=bufs, w=[tb])
                S.dma("sp", dbg_out[name], tmp[:], r=[tb], w=[Buf()], is_out=True)

        ident_f = sb([128, 128], F32); ident_b = sb([128, 128], BF16)
        ones_f = sb([128, 128], F32); ones_b = sb([128, 128], BF16)
        pswap = sb([128, 128], F32)
        epsr = sb([128, 1], F32); epsg = sb([128, 1], F32)
        badaT = sb([128, 96], F32); gnorm = sb([128, 4, 16], F32)
        hsw = sb([128, 24, 3], F32); hbias = sb([128, 2, 8], F32); fcw = sb([128, 88, 3], F32)
        gn_bc = sb([128, 1024], F32)
        modT = sb([128, 96, 2], F32)
        cT = sb([128, 16, 2], F32); sT = sb([128, 16, 2], BF16)
        retc = sb([128, 6, 128], F32); retpc = sb([128, 3], F32)
        lg = sb([128, 16], F32)
        msum = sb([128, 8, 128], F32)
        xi_bc = sb([128, 16, 128], F32)
        zcol = sb([128, 16], F32)
        cdcol = sb([128, 16], F32)
        hmask = sb([128, 2], F32)
        CB = Buf("consts")

        CLOAD = []

        def _cl():
            b_ = Buf(f"cload{len(CLOAD)}")
            CLOAD.append(b_)
            return b_

        S.dma("sp", ident_f[:], din["ident"], w=[_cl()])
        S.dma("sp", pswap[:], din["pswap"], w=[_cl()])
        S.dma("sp", badaT[:], din["badaT"], w=[_cl()])
        S.dma("sp", gnorm[:], din["gnorm"], w=[_cl()])
        S.dma("sp", hsw[:], din["hsw"], w=[_cl()])
        S.dma("sp", hbias[:], din["hbias"], w=[_cl()])
        S.dma("sp", fcw[:], din["fcw"], w=[_cl()])
        S.dma("sp", gn_bc[:], din["retgn"].partition_broadcast(128), w=[_cl()])
        S.dma("sp", lg[:], din["retlogit"].partition_broadcast(128), w=[_cl()])
        S.dma("sp", cT[:], din["cT"], w=[_cl()])
        S.dma("sp", retc[:], din["retc"], w=[_cl()])
        S.dma("sp", retpc[:], din["retpc"], w=[_cl()])
        S.dma("sp", hmask[:], din["hmask"], w=[_cl()])
        S.op("dve", lambda e: e.memset(epsr[:], RMS_EPS), r=CLOAD, w=[CB])
        S.op("dve", lambda e: e.tensor_copy(ident_b[:], ident_f[:]), r=[CB], w=[CB])
        S.op("dve", lambda e: e.memset(ones_f[:], 1.0), w=[CB])
        S.op("dve", lambda e: e.memset(ones_b[:], 1.0), w=[CB])
        S.op("dve", lambda e: e.memset(epsg[:], GN_EPS), w=[CB])

        sf = sb([128, 16, 2], F32)
        S.op("act", lambda e: e.activation(sf[:], cT[:], AF.Silu), r=[CB], w=[CB])
        S.op("dve", lambda e: e.tensor_copy(sT[:], sf[:]), r=[CB], w=[CB])
        S.op("act", lambda e: e.activation(lg[:], lg[:], AF.Exp, scale=-1.0), r=[CB], w=[CB])
        S.op("dve", lambda e: e.tensor_scalar(lg[:], lg[:], 1.0, None, ALU.add), r=[CB], w=[CB])
        S.op("act", lambda e: e.activation(lg[:], lg[:], AF.Ln), r=[CB], w=[CB])
        S.op("dve", lambda e: e.tensor_scalar(lg[:], lg[:], -1.0, None, ALU.mult), r=[CB], w=[CB])
        with ExitStack() as ph:
            e1 = sb([128, 128], F32, ph); e2 = sb([128, 128], F32, ph); xt_ = sb([128, 128], F32, ph)
            TB = Buf("rtmp")
            for h in range(8):
                lf = lg[:, h:h + 1]
                lb = lg[:, 8 + h:9 + h]
                S.op("act", lambda e: e.activation(e1[:], retc[:, 0, :], AF.Exp, scale=lf), r=[CB], w=[TB])
                S.op("dve", lambda e: e.tensor_tensor(e1[:], e1[:], retc[:, 2, :], ALU.mult), r=[CB, TB], w=[TB])
                S.op("act", lambda e: e.activation(e2[:], retc[:, 1, :], AF.Exp, scale=lb), r=[CB], w=[TB])
                S.op("dve", lambda e: e.tensor_tensor(e2[:], e2[:], retc[:, 3, :], ALU.mult), r=[CB, TB], w=[TB])
                S.op("dve", lambda e: e.tensor_tensor(msum[:, h, :], e1[:], e2[:], ALU.add), r=[TB], w=[CB])
                S.op("act", lambda e: e.activation(xi_bc[:, h, :], retc[:, 4, :], AF.Exp, scale=lf), r=[CB], w=[CB])
                S.op("act", lambda e: e.activation(xi_bc[:, 8 + h, :], retc[:, 5, :], AF.Exp, scale=lb), r=[CB], w=[CB])
                S.op("act", lambda e: e.activation(zcol[:, h:h + 1], retpc[:, 0:1], AF.Exp, scale=lf), r=[CB], w=[CB])
                S.op("act", lambda e: e.activation(zcol[:, 8 + h:9 + h], retpc[:, 1:2], AF.Exp, scale=lb), r=[CB], w=[CB])
                S.op("act", lambda e: e.activation(cdcol[:, h:h + 1], retpc[:, 2:3], AF.Exp, scale=lf), r=[CB], w=[CB])
                S.op("act", lambda e: e.activation(cdcol[:, 8 + h:9 + h], retpc[:, 2:3], AF.Exp, scale=lb), r=[CB], w=[CB])
            S.barrier()

        MODB = Buf("modT_late")
        late_mod = [8]

        def mod_cblk(cblk, mrow_t, mrow_b, bacc, btr, outbuf):
            for half in range(2):
                wt, wb = w_take("wada", cblk * 2 + half)
                wv = wt[:, 0:4096].rearrange("p (k n) -> p k n", n=512)
                for k in range(8):
                    kc = half * 8 + k
                    S.op("pe", lambda e: e.matmul(pbank[bacc][0:2, :], sT[:, kc, :], wv[:, k, :],
                                                  start=(kc == 0), stop=(kc == 15)),
                         r=[CB, wb], w=PB(bacc), sig=(k == 7))
            S.op("act", lambda e: e.activation(mrow_t[:], pbank[bacc][0:2, :], AF.Copy), r=PB(bacc), w=[mrow_b])
            for q in range(4):
                S.op("pe", lambda e: e.matmul(pbank[btr][:, q * 2:q * 2 + 2], mrow_t[0:2, q * 128:(q + 1) * 128],
                                              ident_f[0:2, 0:2], start=True, stop=True),
                     r=[mrow_b, CB], w=PB(btr), sig=(q == 3))
            S.op("dve", lambda e: e.tensor_tensor(
                modT[:, cblk * 4:(cblk + 1) * 4, :],
                pbank[btr][:, 0:8].rearrange("p (q v) -> p q v", v=2),
                badaT[:, cblk * 4:(cblk + 1) * 4].unsqueeze(2).to_broadcast([128, 4, 2]), ALU.add),
                r=PB(btr) + [CB], w=[outbuf])

        with ExitStack() as ph:
            mrow = [sb([2, 512], F32, ph) for _ in range(2)]
            MB = [Buf("mrow0"), Buf("mrow1")]
            zb = sb([128, 512], BF16, ph)
            ZBB = Buf("zb")
            S.op("dve", lambda e: e.memset(zb[:], 0.0), w=[ZBB])
            for i_ in range(8):
                S.op("pe", lambda e: e.matmul(pbank[i_][:, :], zb[:, 0:128], zb[:], start=True, stop=True), r=[ZBB], w=PB(i_))
            for cblk in range(8):
                mod_cblk(cblk, mrow[cblk % 2], MB[cblk % 2], cblk % 2, 2 + cblk % 2, CB)
            S.barrier()

        def rstd_from_ss(ss_ap, out_ap, bufs_r, bufs_w, eps_ap, scale):
            S.op("act", lambda e: e.activation(out_ap, ss_ap, AF.Sqrt, bias=eps_ap, scale=scale), r=bufs_r + [CB], w=bufs_w)
            S.op("dve", lambda e: e.reciprocal(out_ap, out_ap), r=bufs_w, w=bufs_w)

        def norm_T(ph, x_rows, dstT, dstB, gs_ap, sh_ap, vecB, col0=0):
            need_x = any(sbuf is None for (_, _, sbuf) in x_rows)
            xin = [sb([128, D], F32, ph) for _ in range(2)] if need_x else None
            xn = [sb([128, D], BF16, ph) for _ in range(2)]
            junk = sb([128, D], BF16, ph)
            ssx = sb([128, 8], F32, ph)
            XB = [Buf("xin0"), Buf("xin1")]
            NB = [Buf("xn0"), Buf("xn1")]
            JB = Buf("junk")
            SSB = Buf("ssx")
            S.op("dve", lambda e: e.memset(ssx[:], 0.0), w=[SSB])
            SST = [Buf(f"ss{i}") for i in range(len(x_rows))]
            cols = []
            c_ = col0
            for (_, rows, _) in x_rows:
                cols.append(c_)
                c_ += rows

            def front(ti):
                src, rows, srcbuf = x_rows[ti]
                b = ti % 2
                if srcbuf is None:
                    S.dma("sp", xin[b][0:rows, :], src, w=[XB[b]])
                    xa, xb_ = xin[b], [XB[b]]
                else:
                    xa, xb_ = src, [srcbuf]
                ssc = ssx[0:rows, ti % 8:ti % 8 + 1]
                S.op("act", lambda e: e.activation(junk[0:rows, :], xa[0:rows, :], AF.Square, accum_out=ssc), r=xb_ + [SSB], w=[JB, SST[ti]])
                rstd_from_ss(ssc, ssc, [SST[ti]], [SST[ti]], epsr[0:rows, :], 1.0 / D)
                S.op("act", lambda e: e.activation(xn[b][0:rows, :], xa[0:rows, :], AF.Copy, scale=ssc), r=xb_ + [SST[ti]], w=[NB[b]])

            def trans(ti):
                _, rows, _ = x_rows[ti]
                b = ti % 2
                for half in range(2):
                    bank = (6 if ti % 2 == 0 else 4) + half
                    for k in range(8):
                        kc = half * 8 + k
                        S.op("pe", lambda e: e.transpose(pbf(bank)[:, k * 128:k * 128 + rows], xn[b][0:rows, kc * 128:(kc + 1) * 128],
                                                         ident_b[0:rows, 0:rows]),
                             r=[NB[b], CB], w=PB(bank), sig=(k == 7))

            def evac(ti):
                _, rows, _ = x_rows[ti]
                col = cols[ti]
                for half in range(2):
                    bank = (6 if ti % 2 == 0 else 4) + half
                    for k in range(8):
                        kc = half * 8 + k
                        S.op("dve", lambda e: e.tensor_scalar(dstT[:, kc, col:col + rows], pbf(bank)[:, k * 128:k * 128 + rows],
                                                              gs_ap[:, kc:kc + 1], sh_ap(kc), ALU.mult, ALU.add),
                             r=PB(bank) + [vecB], w=[dstB])

            n_ = len(x_rows)
            front(0)
            trans(0)
            for ti in range(1, n_):
                front(ti)
                evac(ti - 1)
                trans(ti)
            evac(n_ - 1)

        def conv3(out_ap, raw_ap, wcol, nseg, seglen, rB, wB, raw_is_psum=False):
            S.op("act", lambda e: e.activation(out_ap, raw_ap, AF.Copy, scale=wcol[:, 1:2]), r=rB + [CB], w=wB)
            o3 = out_ap.rearrange("p (s l) -> p s l", l=seglen)
            r3 = raw_ap.rearrange("p (s l) -> p s l", l=seglen)
            S.op("dve", lambda e: e.scalar_tensor_tensor(o3[:, :, 1:seglen], r3[:, :, 0:seglen - 1], wcol[:, 0:1],
                                                         o3[:, :, 1:seglen], ALU.mult, ALU.add), r=rB + wB + [CB], w=wB)
            S.op("dve", lambda e: e.scalar_tensor_tensor(o3[:, :, 0:seglen - 1], r3[:, :, 1:seglen], wcol[:, 2:3],
                                                         o3[:, :, 0:seglen - 1], ALU.mult, ALU.add), r=rB + wB + [CB], w=wB)

        def drive(ga, gb):
            a_done = ga is None
            b_done = gb is None
            while not (a_done and b_done):
                if not b_done:
                    try:
                        next(gb)
                    except StopIteration:
                        b_done = True
                if not a_done:
                    try:
                        next(ga)
                    except StopIteration:
                        a_done = True

        def run_pass(pi):
            sample = (pi == 0)
            v = 0 if sample else 1
            L = LS if sample else LP
            nseq = 1 if sample else 2
            Tm = nseq * L
            NT = Tm // 128
            NCH = Tm // 512
            N = L // 128
            Tw = WIN if sample else 512
            tag = "s" if sample else "p"
            if sample:
                tiles_w = [(0, 128), (128, 128), (256, 2)]
            else:
                tiles_w = [(i * 128, 128) for i in range(4)]
            prow0 = (pi - 1) * 512
            with ExitStack() as pp:
                gsm = sb([128, 16], F32, pp); gsf = sb([128, 16], F32, pp)
                gvm = sb([128, 16], F32, pp); gvf = sb([128, 16], F32, pp)
                VB = Buf("vecs")
                S.op("dve", lambda e: e.scalar_tensor_tensor(gsm[:], modT[:, 16:32, v], 1.0, gnorm[:, 0, :], ALU.add, ALU.mult), r=[CB], w=[VB])
                VB2 = Buf("vecs2")

                def late_vecs():
                    S.op("dve", lambda e: e.scalar_tensor_tensor(gsf[:], modT[:, 64:80, v], 1.0, gnorm[:, 1, :], ALU.add, ALU.mult), r=[CB, MODB], w=[VB2])
                    S.op("dve", lambda e: e.tensor_tensor(gvm[:], modT[:, 32:48, v], gnorm[:, 2, :], ALU.mult), r=[CB, MODB], w=[VB2])
                    S.op("dve", lambda e: e.tensor_tensor(gvf[:], modT[:, 80:96, v], gnorm[:, 3, :], ALU.mult), r=[CB, MODB], w=[VB2])
                shm = lambda kc: modT[:, kc, v:v + 1]
                shf = lambda kc: modT[:, 48 + kc, v:v + 1]

                pm = ExitStack()
                pr = ExitStack()
                pp.callback(pr.close)
                pp.callback(pm.close)
                hT = sb([128, 16, Tm], BF16, pm)
                HB = Buf("hT")
                yhyTw = sb([128, 8, Tw], BF16, pm); yretTw = sb([128, 8, Tw], BF16, pm)
                YHB = Buf("yhyTw"); YRB = Buf("yretTw")

                with ExitStack() as ph:
                    if sample:
                        rows = [(din["xs_full"][i * 128:(i + 1) * 128, :], 128, None) for i in range(NT)]
                    else:
                        rows = [(din["xp"][prow0 + i * 128:prow0 + (i + 1) * 128, :], 128, None) for i in range(NT)]
                    norm_T(ph, rows, hT, HB, gsm, shm, VB)
                    S.barrier()
                dump(f"hT{pi}", hT[:, :, 0:128], [HB])
                if stop_after == "norm1":
                    return

                with ExitStack() as ph:
                    if sample:
                        ropeC = sb([128, LS], F32, ph); ropeS = sb([128, LS], F32, ph)
                        sel = sb([128, 8, WIN], BF16, ph)
                        st0 = sb([128, 16, 128], F32, ph)
                        S.join("pool")
                        S.dma("sp", ropeC[:], din["ropeC"], w=[CB])
                        S.dma("sp", ropeS[:], din["ropeS"], w=[CB])
                        S.dma("pool", sel[:], din["sel"].rearrange("p (t w) -> p t w", w=WIN), w=[CB])
                        S.dma("sp", st0[:], din["state0"].rearrange("d h p e -> p (d h) e"), w=[CB])
                        QTf = sb([128, Tm], F32, ph); KTf = sb([128, Tm], F32, ph)
                        t1 = sb([128, 512], F32, ph); t2 = sb([128, 512], F32, ph)
                    QFB, KFB, T1B, T2B = Buf("QTf"), Buf("KTf"), Buf("t1"), Buf("t2")

                    def mk_ws():
                        w_ = {}
                        w_["QT"] = sb([128, Tm], BF16, ph); w_["KT"] = sb([128, Tm], BF16, ph)
                        w_["Qxf"] = sb([128, Tm], BF16, ph); w_["Qxb"] = sb([128, Tm], BF16, ph)
                        w_["Vh"] = sb([128, NT, 128], BF16, ph); w_["Gh"] = sb([128, NT, 128], BF16, ph)
                        w_["Kzf"] = sb([128, NT, 128], BF16, ph); w_["Kzb"] = sb([128, NT, 128], BF16, ph)
                        w_["Rf"] = [sb([128, 128], F32, ph) for _ in range(2)]
                        w_["Rb"] = [sb([128, 128], F32, ph) for _ in range(2)]
                        w_["Rfb"] = sb([128, NT, 128], BF16, ph); w_["Rbb"] = sb([128, NT, 128], BF16, ph)
                        w_["Sm"] = sb([128, NT, 128], BF16, ph)
                        w_["on"] = sb([128, NT, 128], F32, ph)
                        w_["st6"] = sb([128, NT, 6], F32, ph); w_["mv"] = sb([128, NT, 2], F32, ph); w_["rsd"] = sb([128, NT], F32, ph)
                        w_["yr"] = sb([128, NT, 128], BF16, ph)
                        for nm in ("QB", "KB_", "QXB", "VB_", "GB_", "KZB", "RFbB", "RBbB", "ONB", "STB", "YRtB"):
                            w_[nm] = Buf(nm)
                        w_["RFB"] = [Buf("Rf0"), Buf("Rf1")]; w_["RBB"] = [Buf("Rb0"), Buf("Rb1")]
                        w_["SMB"] = Buf("SmA")
                        return w_

                    WS = [mk_ws(), mk_ws()]
                    stg = [sb([128, 128], F32, ph) for _ in range(6)]
                    STGB = [Buf(f"stg{i}") for i in range(6)]
                    stg_i = [0]
                    def ret_A(h):
                        w_ = WS[h % 2]
                        QT, KT, Qxf, Qxb, Vh, Gh, Kzf, Kzb = (w_[k] for k in ("QT", "KT", "Qxf", "Qxb", "Vh", "Gh", "Kzf", "Kzb"))
                        Rf, Rb, Rfb, Rbb, SmA, on, st6, mv, rsd, yr = (w_[k] for k in ("Rf", "Rb", "Rfb", "Rbb", "Sm", "on", "st6", "mv", "rsd", "yr"))
                        QB, KB_, QXB, VB_, GB_, KZB, RFbB, RBbB, ONB, STB, YRtB = (w_[k] for k in ("QB", "KB_", "QXB", "VB_", "GB_", "KZB", "RFbB", "RBbB", "ONB", "STB", "YRtB"))
                        RFB, RBB, SMB = w_["RFB"], w_["RBB"], w_["SMB"]
                        wq, wqb = w_take("wqk", h)
                        wqv = wq[:, 0:4096].rearrange("p (k n) -> p k n", n=256)
                        for which in range(2):
                            for c in range(NCH):
                                bank = (which * NCH + c) % 2
                                for kc in range(16):
                                    S.op("pe", lambda e: e.matmul(pbank[bank][:, :], wqv[:, kc, which * 128:(which + 1) * 128],
                                                                  hT[:, kc, c * 512:(c + 1) * 512], start=(kc == 0), stop=(kc == 15)),
                                         r=[wqb, HB], w=PB(bank), sig=(kc == 15))
                                cs = slice(c * 512, (c + 1) * 512)
                                scl = 1.0 if which == 0 else 128.0 ** -0.5
                                if not sample:
                                    dst, dB = (QT, QB) if which == 0 else (KT, KB_)
                                    S.op("act", lambda e: e.activation(dst[:, cs], pbank[bank][:, :], AF.Copy, scale=scl), r=PB(bank), w=[dB])
                                    yield
                                else:
                                    dstf, dfB = (QTf, QFB) if which == 0 else (KTf, KFB)
                                    dst, dB = (QT, QB) if which == 0 else (KT, KB_)
                                    S.op("act", lambda e: e.activation(dstf[:, cs], pbank[bank][:, :], AF.Copy, scale=scl), r=PB(bank), w=[dfB])
                                    S.op("pe", lambda e: e.matmul(pbank[2][:, :], pswap[:], dstf[:, cs], start=True, stop=True),
                                         r=[CB, dfB], w=PB(2))
                                    S.op("dve", lambda e: e.tensor_tensor(t1[:], dstf[:, cs], ropeC[:, cs], ALU.mult), r=[dfB, CB], w=[T1B])
                                    S.op("dve", lambda e: e.tensor_tensor(t2[:], pbank[2][:, :], ropeS[:, cs], ALU.mult), r=PB(2) + [CB], w=[T2B])
                                    S.op("dve", lambda e: e.tensor_tensor(dst[:, cs], t1[:], t2[:], ALU.add), r=[T1B, T2B], w=[dB])
                                    yield
                        wv_, wvb = w_take("wvg", h)
                        wvv = wv_[:, 0:4096].rearrange("p (k n) -> p k n", n=256)
                        for g_ in range(NT):
                            bank = (g_ % 2) if sample else (3 if g_ % 2 == 0 else 2)
                            q = 0
                            for kc in range(16):
                                S.op("pe", lambda e: e.matmul(pbank[bank][:, q * 256:(q + 1) * 256], hT[:, kc, g_ * 128:(g_ + 1) * 128],
                                                              wvv[:, kc, :], start=(kc == 0), stop=(kc == 15)),
                                     r=[wvb, HB], w=PB(bank, q * 256, (q + 1) * 256), sig=(kc == 15))
                            S.op("act", lambda e: e.activation(Vh[:, g_, :], pbank[bank][:, q * 256:q * 256 + 128], AF.Copy),
                                 r=PB(bank, q * 256, q * 256 + 128), w=[VB_])
                            S.op("act", lambda e: e.activation(Gh[:, g_, :], pbank[bank][:, q * 256 + 128:(q + 1) * 256], AF.Silu),
                                 r=PB(bank, q * 256 + 128, (q + 1) * 256), w=[GB_])
                            yield
                        if h == 0:
                            dump(f"QT{pi}", QT[:, 0:128], [QB]); dump(f"KT{pi}", KT[:, 0:128], [KB_])
                            dump(f"Vh{pi}", Vh[:, 0, :], [VB_]); dump(f"Gh{pi}", Gh[:, 0, :], [GB_])
                        QT3 = QT[:].rearrange("p (g i) -> p g i", i=128)
                        S.op("dve", lambda e: e.tensor_tensor(Qxf[:].rearrange("p (g i) -> p g i", i=128), QT3,
                                                              xi_bc[:, h, :].unsqueeze(1).to_broadcast([128, NT, 128]), ALU.mult), r=[QB, CB], w=[QXB])
                        S.op("dve", lambda e: e.tensor_tensor(Qxb[:].rearrange("p (g i) -> p g i", i=128), QT3,
                                                              xi_bc[:, 8 + h, :].unsqueeze(1).to_broadcast([128, NT, 128]), ALU.mult), r=[QB, CB], w=[QXB])
                        yield
                        for g_ in range(NT):
                            S.op("pe", lambda e: e.transpose(pbf(6)[:, g_ * 128:(g_ + 1) * 128], KT[:, g_ * 128:(g_ + 1) * 128], ident_b[:]),
                                 r=[KB_, CB], w=PB(6), sig=(g_ == NT - 1))
                        S.op("act", lambda e: e.activation(Kzf[:].rearrange("p g d -> p (g d)"), pbf(6)[:, 0:NT * 128], AF.Copy, scale=zcol[:, h:h + 1]),
                             r=PB(6) + [CB], w=[KZB])
                        S.op("act", lambda e: e.activation(Kzb[:].rearrange("p g d -> p (g d)"), pbf(6)[:, 0:NT * 128], AF.Copy, scale=zcol[:, 8 + h:9 + h]),
                             r=PB(6) + [CB], w=[KZB])
                        yield
                        yield
                    def ret_B(h):
                        w_ = WS[h % 2]
                        QT, KT, Qxf, Qxb, Vh, Gh, Kzf, Kzb = (w_[k] for k in ("QT", "KT", "Qxf", "Qxb", "Vh", "Gh", "Kzf", "Kzb"))
                        Rf, Rb, Rfb, Rbb, SmA, on, st6, mv, rsd, yr = (w_[k] for k in ("Rf", "Rb", "Rfb", "Rbb", "Sm", "on", "st6", "mv", "rsd", "yr"))
                        QB, KB_, QXB, VB_, GB_, KZB, RFbB, RBbB, ONB, STB, YRtB = (w_[k] for k in ("QB", "KB_", "QXB", "VB_", "GB_", "KZB", "RFbB", "RBbB", "ONB", "STB", "YRtB"))
                        RFB, RBB, SMB = w_["RFB"], w_["RBB"], w_["SMB"]
                        for d_ in range(2):
                            Kz = Kzf if d_ == 0 else Kzb
                            R, RB_ = (Rf, RFB) if d_ == 0 else (Rb, RBB)
                            Rbf, RbfB = (Rfb, RFbB) if d_ == 0 else (Rbb, RBbB)
                            cd = cdcol[:, d_ * 8 + h:d_ * 8 + h + 1]
                            for g_ in range(NT):
                                bank = 4 + (g_ // 4) % 2
                                q = g_ % 4
                                S.op("pe", lambda e: e.matmul(pbank[bank][:, q * 128:(q + 1) * 128], Kz[:, g_, :], Vh[:, g_, :], start=True, stop=True),
                                     r=[KZB, VB_], w=PB(bank, q * 128, (q + 1) * 128))
                            for s_ in range(nseq):
                                cur = 0
                                if sample:
                                    S.op("dve", lambda e: e.tensor_copy(R[0][:], st0[:, d_ * 8 + h, :]), r=[CB], w=[RB_[0]])
                                else:
                                    S.op("dve", lambda e: e.memset(R[0][:], 0.0), w=[RB_[0]])
                                order = list(range(N)) if d_ == 0 else list(range(N - 1, -1, -1))
                                for c in order:
                                    g_ = s_ * N + c
                                    bank = 4 + (g_ // 4) % 2
                                    q = g_ % 4
                                    S.op("act", lambda e: e.activation(Rbf[:, g_, :], R[cur][:], AF.Copy), r=[RB_[cur]], w=[RbfB])
                                    if c == order[-1] and not sample:
                                        k_ = stg_i[0] % len(stg); stg_i[0] += 1
                                        rdst, rdB = stg[k_], STGB[k_]
                                    else:
                                        rdst, rdB = R[1 - cur], RB_[1 - cur]
                                    S.op("dve", lambda e: e.scalar_tensor_tensor(rdst[:], R[cur][:], cd, pbank[bank][:, q * 128:(q + 1) * 128],
                                                                                 ALU.mult, ALU.add),
                                         r=[RB_[cur], CB] + PB(bank, q * 128, (q + 1) * 128), w=[rdB])
                                    yield
                                    cur = 1 - cur
                                if not sample:
                                    S.dma("sp", stout[(pi - 1) * 2 + s_, d_, h], rdst[:], r=[rdB], w=[Buf()], is_out=True)
                        sbks = (3, 5) if not sample else (3, 3)
                        for g_ in range(NT):
                            sbk = sbks[g_ % 2]
                            S.op("pe", lambda e: e.matmul(pbank[sbk][:, 0:128], KT[:, g_ * 128:(g_ + 1) * 128], QT[:, g_ * 128:(g_ + 1) * 128],
                                                          start=True, stop=True), r=[KB_, QB], w=PB(sbk))
                            S.op("dve", lambda e: e.tensor_tensor(SmA[:, g_, :], pbank[sbk][:, 0:128], msum[:, h, :], ALU.mult),
                                 r=PB(sbk) + [CB], w=[SMB])
                            yield
                        for g_ in range(NT):
                            ob = 7 if g_ < 4 else 4
                            oc = slice((g_ % 4) * 128, (g_ % 4 + 1) * 128)
                            S.op("pe", lambda e: e.matmul(pbank[ob][:, oc], SmA[:, g_, :], Vh[:, g_, :], start=True, stop=False),
                                 r=[SMB, VB_], w=PB(ob), sig=False)
                            S.op("pe", lambda e: e.matmul(pbank[ob][:, oc], Qxf[:, g_ * 128:(g_ + 1) * 128], Rfb[:, g_, :], start=False, stop=False),
                                 r=[QXB, RFbB], w=PB(ob), sig=False)
                            S.op("pe", lambda e: e.matmul(pbank[ob][:, oc], Qxb[:, g_ * 128:(g_ + 1) * 128], Rbb[:, g_, :], start=False, stop=True),
                                 r=[QXB, RBbB], w=PB(ob))
                            if g_ % 4 == 3:
                                yield
                        if h == 0:
                            dump(f"oraw{pi}", pbank[7][:, 0:128], PB(7))
                        for g_ in range(NT):
                            ob = 7 if g_ < 4 else 4
                            oc = slice((g_ % 4) * 128, (g_ % 4 + 1) * 128)
                            S.op("dve", lambda e: e.bn_stats(st6[:, g_, :], pbank[ob][:, oc]), r=PB(ob), w=[STB])
                            S.op("dve", lambda e: e.bn_aggr(mv[:, g_, :], st6[:, g_, :]), r=[STB], w=[STB])
                            if g_ % 2 == 1:
                                yield
                        rstd_from_ss(mv[:, :, 1], rsd[:, :], [STB], [STB], epsg[:], 1.0)
                        yield
                        for g_ in range(NT):
                            ob = 7 if g_ < 4 else 4
                            oc = slice((g_ % 4) * 128, (g_ % 4 + 1) * 128)
                            S.op("dve", lambda e: e.tensor_scalar(on[:, g_, :], pbank[ob][:, oc], mv[:, g_, 0:1], rsd[:, g_:g_ + 1], ALU.subtract, ALU.mult),
                                 r=PB(ob) + [STB], w=[ONB])
                            if g_ % 2 == 1:
                                yield
                        S.op("dve", lambda e: e.tensor_tensor(on[:], on[:], gn_bc[:, h * 128:(h + 1) * 128].unsqueeze(1).to_broadcast([128, NT, 128]), ALU.mult),
                             r=[ONB, CB], w=[ONB])
                        S.op("dve", lambda e: e.tensor_tensor(yr[:], on[:], Gh[:], ALU.mult), r=[ONB, GB_], w=[YRtB])
                        yield
                        if h == 0:
                            dump(f"yr{pi}", yr[:, 0, :], [YRtB])
                        if sample:
                            for g_ in range(NT):
                                S.op("pe", lambda e: e.matmul(pbank[2][:, 0:WIN], yr[:, g_, :], sel[:, g_, :], start=(g_ == 0), stop=(g_ == NT - 1)),
                                     r=[YRtB, CB], w=PB(2), sig=(g_ == NT - 1))
                            S.op("act", lambda e: e.activation(yretTw[:, h, :], pbank[2][:, 0:WIN], AF.Copy), r=PB(2), w=[YRB])
                        else:
                            for g_ in range(NT):
                                S.op("pe", lambda e: e.transpose(pbf(6)[:, g_ * 128:(g_ + 1) * 128], yr[:, g_, :], ident_b[:]),
                                     r=[YRtB, CB], w=PB(6), sig=(g_ == NT - 1))
                            S.op("act", lambda e: e.activation(yretTw[:, h, :], pbf(6)[:, 0:NT * 128], AF.Copy), r=PB(6), w=[YRB])
                        yield


                    mrowL = [sb([2, 512], F32, ph) for _ in range(2)]
                    MBL = [Buf("mrowL0"), Buf("mrowL1")]
                    drive(ret_A(0), None)
                    for h in range(8):
                        drive(ret_A(h + 1) if h + 1 < 8 else None, ret_B(h))
                        if late_mod[0] < 24:
                            for _ in range(2):
                                cb_ = late_mod[0]
                                mod_cblk(cb_, mrowL[cb_ % 2], MBL[cb_ % 2], 2, 6, MODB)
                                late_mod[0] += 1
                    S.barrier()
                dump(f"yretTw{pi}", yretTw[:, 0, 0:128], [YRB])
                if stop_after == "ret":
                    return

                with ExitStack() as ph:
                    nt = N
                    Cm = sb([128, nt, L], BF16, ph); Sn = sb([128, nt, L], BF16, ph)
                    nyqc = sb([128, nt], BF16, ph); nyqr = sb([1, L], BF16, ph)
                    wfc = sb([128, nt], F32, ph); negt = sb([128, nt], F32, ph)
                    DB = Buf("dftc")
                    S.join("pool")
                    S.dma("pool", Cm[:], din[f"dft{tag}_C"].rearrange("p (t f) -> p t f", f=L), w=[DB])
                    S.dma("pool", Sn[:], din[f"dft{tag}_S"].rearrange("p (t f) -> p t f", f=L), w=[DB])
                    S.dma("pool", nyqc[:], din[f"dft{tag}_nyq_col"], w=[DB])
                    S.dma("pool", nyqr[:], din[f"dft{tag}_nyq_row"], w=[DB])
                    S.dma("sp", wfc[:], din[f"dft{tag}_wf"], w=[DB])
                    S.dma("sp", negt[:], din[f"dft{tag}_negt"], w=[DB])
                    if sample:
                        sel = sb([128, 8, WIN], BF16, ph)
                        S.dma("pool", sel[:], din["sel"].rearrange("p (t w) -> p t w", w=WIN), w=[DB])
                    h2T = sb([65, L], F32, ph)
                    H2B = Buf("h2T")
                    with ExitStack() as ph2_:
                        ph2 = ph2_ if sample else ph
                        zT = sb([33, L], F32, ph2); w1 = sb([33, 64], F32, ph2); w2 = sb([64, 64], F32, ph2)
                        b12 = sb([64, 2], F32, ph2); fr = sb([64, 2], F32, ph2); fb = sb([64, 2], F32, ph2)
                        h1T = sb([64, L], F32, ph2); arg = sb([64, 512], F32, ph2); kk = sb([64, 512], F32, ph2)
                        MLB = Buf("mlp"); AB = Buf("arg"); H1B = Buf("h1T")
                        S.dma("sp", zT[:], din[f"dft{tag}_zT"], w=[MLB])
                        S.dma("sp", w1[:], din["hyw1"], w=[MLB])
                        S.dma("sp", w2[:], din["hyw2"], w=[MLB])
                        S.dma("sp", b12[:], din["hyb12"], w=[MLB])
                        S.dma("sp", fr[:], din["hyfreq"], w=[MLB])
                        S.op("dve", lambda e: e.tensor_tensor(fb[:], fr[:], b12[:], ALU.mult), r=[MLB], w=[MLB])
                        S.op("dve", lambda e: e.memset(h2T[64:65, :], 1.0), w=[H2B])
                        for layer in range(2):
                            for c in range((L + 511) // 512):
                                n_ = min(512, L - c * 512)
                                cs = slice(c * 512, c * 512 + n_)
                                if layer == 0:
                                    S.op("pe", lambda e: e.matmul(pbank[0][0:64, 0:n_], w1[:], zT[:, cs], start=True, stop=True), r=[MLB], w=PB(0))
                                else:
                                    S.op("pe", lambda e: e.matmul(pbank[0][0:64, 0:n_], w2[:], h1T[:, cs], start=True, stop=True), r=[MLB, H1B], w=PB(0))
                                S.op("dve", lambda e: e.tensor_scalar(arg[:, 0:n_], pbank[0][0:64, 0:n_], fr[:, layer:layer + 1], fb[:, layer:layer + 1],
                                                                      ALU.mult, ALU.add), r=PB(0) + [MLB], w=[AB])
                                S.op("dve", lambda e: e.tensor_scalar(kk[:, 0:n_], arg[:, 0:n_], 1.0 / TWO_PI, MAGIC, ALU.mult, ALU.add), r=[AB], w=[AB])
                                S.op("dve", lambda e: e.tensor_scalar(kk[:, 0:n_], kk[:, 0:n_], -MAGIC, None, ALU.add), r=[AB], w=[AB])
                                S.op("dve", lambda e: e.scalar_tensor_tensor(arg[:, 0:n_], kk[:, 0:n_], -TWO_PI, arg[:, 0:n_], ALU.mult, ALU.add), r=[AB], w=[AB])
                                S.op("dve", lambda e: e.tensor_scalar(arg[:, 0:n_], arg[:, 0:n_], math.pi, -math.pi, ALU.min, ALU.max), r=[AB], w=[AB])
                                if layer == 0:
                                    S.op("act", lambda e: e.activation(h1T[:, cs], arg[:, 0:n_], AF.Sin), r=[AB], w=[H1B])
                                else:
                                    S.op("act", lambda e: e.activation(h2T[0:64, cs], arg[:, 0:n_], AF.Sin), r=[AB], w=[H2B])
                        if sample:
                            S.barrier()
                    dump(f"h2T{pi}", h2T[0:64, 0:128], [H2B])

                    def mk_hws():
                        w_ = {}
                        w_["w3s"] = sb([65, 512], F32, ph); w_["adec"] = sb([128, 512], F32, ph)
                        w_["Et"] = sb([128, 512], F32, ph); w_["hts"] = sb([128, 512], F32, ph); w_["habs"] = sb([128, 512], BF16, ph)
                        w_["hsum"] = sb([128, 512], F32, ph)
                        w_["s_bf"] = sb([128, nt, 256], BF16, ph); w_["d_bf"] = sb([128, nt, 256], BF16, ph)
                        w_["rn"] = sb([128, 512], F32, ph); w_["tmpn"] = w_["Et"][:, 0:256]
                        w_["Ksp"] = sb([128, nt, 512], BF16, ph); w_["Knyq"] = sb([1, 256], F32, ph)
                        w_["raw"] = sb([128, Tm], F32, ph)
                        w_["x1c"] = sb([128, Tm], BF16, ph); w_["x2c"] = sb([128, Tm], BF16, ph); w_["uu"] = sb([128, Tm], F32, ph)
                        w_["u_bf"] = sb([128, Tm], BF16, ph)
                        w_["zz"] = sb([128, Tm], F32, ph)
                        w_["u_tm"] = sb([128, NT, 128], BF16, ph)
                        w_["ta"] = sb([128, 512], F32, ph); w_["tb_"] = sb([128, 512], F32, ph)
                        w_["Yre"] = sb([128, nt, 128], BF16, ph); w_["Yim"] = sb([128, nt, 128], BF16, ph); w_["Ynq"] = sb([1, 128], BF16, ph)
                        w_["yT"] = w_["u_bf"]
                        for nm in ("W3B", "ADB", "ETB", "HTB", "HAB", "SDB", "RNB", "KSB", "RAWB", "X1B", "X2B", "UB", "UBB", "UTB", "TAB", "TBB", "YB", "YTB", "ZB"):
                            w_[nm] = Buf(nm)
                        return w_

                    if sample:
                        h0_ = mk_hws()
                        h1_ = dict(h0_)
                        h1_["Ksp"] = sb([128, nt, 512], BF16, ph); h1_["Knyq"] = sb([1, 256], F32, ph)
                        h1_["x1c"] = sb([128, Tm], BF16, ph); h1_["x2c"] = sb([128, Tm], BF16, ph); h1_["uu"] = sb([128, Tm], F32, ph)
                        for nm in ("KSB", "X1B", "X2B", "UB"):
                            h1_[nm] = Buf(nm + "b")
                        HWS = [h0_, h1_]
                    else:
                        HWS = [mk_hws(), mk_hws()]
                    def hy_A(cb):
                        w_ = HWS[cb % len(HWS)]
                        w3s, adec, Et, hts, habs, s_bf, d_bf, rn, tmpn, Ksp, Knyq, raw = (w_[k] for k in ("w3s", "adec", "Et", "hts", "habs", "s_bf", "d_bf", "rn", "tmpn", "Ksp", "Knyq", "raw"))
                        x1c, x2c, uu, u_bf, u_tm, ta, tb_, Yre, Yim, Ynq, yT = (w_[k] for k in ("x1c", "x2c", "uu", "u_bf", "u_tm", "ta", "tb_", "Yre", "Yim", "Ynq", "yT"))
                        zz = w_["zz"]
                        hsum = w_["hsum"]
                        W3B, ADB, ETB, HTB, HAB, SDB, RNB, KSB, RAWB = (w_[k] for k in ("W3B", "ADB", "ETB", "HTB", "HAB", "SDB", "RNB", "KSB", "RAWB"))
                        X1B, X2B, UB, UBB, UTB, TAB, TBB, YB, YTB = (w_[k] for k in ("X1B", "X2B", "UB", "UBB", "UTB", "TAB", "TBB", "YB", "YTB"))
                        ZB = w_["ZB"]; YTB = UBB
                        S.dma("sp", w3s[:], din["w3b"][cb], w=[W3B])
                        S.dma("sp", adec[:], din["decb"][cb:cb + 1, :].partition_broadcast(128), w=[ADB])
                        S.op("dve", lambda e: e.scalar_tensor_tensor(adec[:], adec[:], -1.0, adec[:], ALU.mult, ALU.max), r=[ADB], w=[ADB])
                        for tt in range(nt):
                            S.op("pe", lambda e: e.matmul(pbank[7][:, :], h2T[:, tt * 128:(tt + 1) * 128], w3s[:], start=True, stop=True),
                                 r=[H2B, W3B], w=PB(7))
                            S.op("act", lambda e: e.activation(Et[:], adec[:], AF.Exp, scale=negt[:, tt:tt + 1]), r=[ADB, DB], w=[ETB])
                            S.op("dve", lambda e: e.tensor_tensor(hts[:], pbank[7][:, :], Et[:], ALU.mult), r=PB(7) + [ETB], w=[HTB])
                            if tt == 0:
                                S.op("dve", lambda e: e.memset(hts[0:1, 256:512], 0.0), w=[HTB])
                                S.op("act", lambda e: e.activation(hsum[:], hts[:], AF.Abs), r=[HTB], w=[HAB])
                            else:
                                S.op("act", lambda e: e.activation(habs[:], hts[:], AF.Abs), r=[HTB], w=[HAB])
                                S.op("dve", lambda e: e.tensor_tensor(hsum[:], hsum[:], habs[:], ALU.add), r=[HAB], w=[HAB])
                            S.op("dve", lambda e: e.tensor_tensor(s_bf[:, tt, :], hts[:, 0:256], hts[:, 256:512], ALU.add), r=[HTB], w=[SDB])
                            S.op("dve", lambda e: e.tensor_tensor(d_bf[:, tt, :], hts[:, 0:256], hts[:, 256:512], ALU.subtract), r=[HTB], w=[SDB])
                            yield
                        S.op("dve", lambda e: e.tensor_tensor(tmpn, hsum[:, 0:256], hsum[:, 256:512], ALU.add), r=[HAB], w=[RNB, ETB])
                        S.op("pe", lambda e: e.matmul(pbank[7][:, 0:256], ones_f[:], tmpn, start=True, stop=True), r=[CB, RNB, ETB], w=PB(7))
                        S.op("dve", lambda e: e.tensor_scalar(rn[:, 0:256], pbank[7][:, 0:256], FILTER_EPS, None, ALU.add), r=PB(7), w=[RNB])
                        S.op("dve", lambda e: e.reciprocal(rn[:, 0:256], rn[:, 0:256]), r=[RNB], w=[RNB])
                        S.op("dve", lambda e: e.tensor_copy(rn[:, 256:512], rn[:, 0:256]), r=[RNB], w=[RNB])
                        for ft in range(nt):
                            bank = 7 - ft % 2
                            for tt in range(nt):
                                S.op("pe", lambda e: e.matmul(pbank[bank][:, 0:256], Cm[:, tt, ft * 128:(ft + 1) * 128], s_bf[:, tt, :],
                                                              start=(tt == 0), stop=(tt == nt - 1)), r=[DB, SDB], w=PB(bank, 0, 256), sig=(tt == nt - 1))
                            for tt in range(nt):
                                S.op("pe", lambda e: e.matmul(pbank[bank][:, 256:512], Sn[:, tt, ft * 128:(ft + 1) * 128], d_bf[:, tt, :],
                                                              start=(tt == 0), stop=(tt == nt - 1)), r=[DB, SDB], w=PB(bank, 256, 512), sig=(tt == nt - 1))
                            S.op("dve", lambda e: e.scalar_tensor_tensor(Ksp[:, ft, :], pbank[bank][:, :], wfc[:, ft:ft + 1], rn[:], ALU.mult, ALU.mult),
                                 r=PB(bank) + [DB, RNB], w=[KSB])
                            yield
                        for tt in range(nt):
                            S.op("pe", lambda e: e.matmul(pbank[7][0:1, 0:256], nyqc[:, tt:tt + 1], s_bf[:, tt, :], start=(tt == 0), stop=(tt == nt - 1)),
                                 r=[DB, SDB], w=PB(7), sig=(tt == nt - 1))
                        S.op("dve", lambda e: e.scalar_tensor_tensor(Knyq[:], pbank[7][0:1, 0:256], 1.0 / (2 * L), rn[0:1, 0:256], ALU.mult, ALU.mult),
                             r=PB(7) + [RNB], w=[KSB])
                        if cb == 0:
                            dump(f"Ksp{pi}", Ksp[:, 0, :], [KSB])

                        wtA, wbA = w_take("whyA", cb)
                        wvA = wtA[:, 0:4096].rearrange("p (k n) -> p k n", n=256)
                        for part in range(3):
                            if part == 2:
                                wtB, wbB = w_take("whyB", cb)
                                wvB = wtB[:, 0:2048].rearrange("p (k n) -> p k n", n=128)
                            wvp, wb = (wvA, wbA) if part < 2 else (wvB, wbB)
                            pc0 = part * 128 if part < 2 else 0
                            for c in range(NCH):
                                bank = (part * NCH + c) % 2
                                for kc in range(16):
                                    S.op("pe", lambda e: e.matmul(pbank[bank][:, :], wvp[:, kc, pc0:pc0 + 128], hT[:, kc, c * 512:(c + 1) * 512],
                                                                  start=(kc == 0), stop=(kc == 15)), r=[wb, HB], w=PB(bank), sig=(kc == 15))
                                S.op("act", lambda e: e.activation(raw[:, c * 512:(c + 1) * 512], pbank[bank][:, :], AF.Copy), r=PB(bank), w=[RAWB])
                                yield
                            conv3(uu[:], raw[:], hsw[:, part * 8 + cb, :], nseq, L, [RAWB], [UB])
                            yield
                            if part < 2:
                                dst, dB = [(x1c, X1B), (x2c, X2B)][part]
                                S.op("act", lambda e: e.activation(dst[:], uu[:], AF.Copy), r=[UB], w=[dB])
                        if cb == 0:
                            dump(f"uu{pi}", uu[:, 0:128], [UB])
                        yield
                    def hy_B(cb):
                        w_ = HWS[cb % len(HWS)]
                        w3s, adec, Et, hts, habs, s_bf, d_bf, rn, tmpn, Ksp, Knyq, raw = (w_[k] for k in ("w3s", "adec", "Et", "hts", "habs", "s_bf", "d_bf", "rn", "tmpn", "Ksp", "Knyq", "raw"))
                        x1c, x2c, uu, u_bf, u_tm, ta, tb_, Yre, Yim, Ynq, yT = (w_[k] for k in ("x1c", "x2c", "uu", "u_bf", "u_tm", "ta", "tb_", "Yre", "Yim", "Ynq", "yT"))
                        zz = w_["zz"]
                        hsum = w_["hsum"]
                        W3B, ADB, ETB, HTB, HAB, SDB, RNB, KSB, RAWB = (w_[k] for k in ("W3B", "ADB", "ETB", "HTB", "HAB", "SDB", "RNB", "KSB", "RAWB"))
                        X1B, X2B, UB, UBB, UTB, TAB, TBB, YB, YTB = (w_[k] for k in ("X1B", "X2B", "UB", "UBB", "UTB", "TAB", "TBB", "YB", "YTB"))
                        ZB = w_["ZB"]; YTB = UBB
                        for o_ in range(2):
                            src, sB = (uu, UB) if o_ == 0 else (zz, ZB)
                            gate, gB = (x1c, X1B) if o_ == 0 else (x2c, X2B)
                            S.op("act", lambda e: e.activation(u_bf[:], src[:], AF.Copy), r=[sB], w=[UBB])
                            for g_ in range(NT):
                                S.op("pe", lambda e: e.transpose(pbf(6 + (g_ // 8) % 2)[:, (g_ % 8) * 128:(g_ % 8 + 1) * 128], u_bf[:, g_ * 128:(g_ + 1) * 128], ident_b[:]),
                                     r=[UBB, CB], w=PB(6), sig=(g_ == NT - 1))
                            S.op("act", lambda e: e.activation(u_tm[:].rearrange("p g c -> p (g c)"), pbf(6)[:, 0:NT * 128], AF.Copy), r=PB(6), w=[UTB])
                            yield
                            for s_ in range(nseq):
                                for fg in range((nt + 3) // 4):
                                    nf_ = min(4, nt - fg * 4)
                                    for fi in range(nf_):
                                        ft = fg * 4 + fi
                                        for tt in range(nt):
                                            S.op("pe", lambda e: e.matmul(pbank[2][:, fi * 128:(fi + 1) * 128], Cm[:, tt, ft * 128:(ft + 1) * 128], u_tm[:, s_ * nt + tt, :],
                                                                          start=(tt == 0), stop=(tt == nt - 1)), r=[DB, UTB], w=PB(2, fi * 128, (fi + 1) * 128), sig=(tt == nt - 1))
                                        for tt in range(nt):
                                            S.op("pe", lambda e: e.matmul(pbank[3][:, fi * 128:(fi + 1) * 128], Sn[:, tt, ft * 128:(ft + 1) * 128], u_tm[:, s_ * nt + tt, :],
                                                                          start=(tt == 0), stop=(tt == nt - 1)), r=[DB, UTB], w=PB(3, fi * 128, (fi + 1) * 128), sig=(tt == nt - 1))
                                    w_ = nf_ * 128
                                    fs = slice(fg * 4, fg * 4 + nf_)
                                    Ure = pbank[2][:, 0:w_].rearrange("p (f c) -> p f c", c=128)
                                    Uim = pbank[3][:, 0:w_].rearrange("p (f c) -> p f c", c=128)
                                    Kre = Ksp[:, fs, o_ * 128:(o_ + 1) * 128]
                                    Kim = Ksp[:, fs, 256 + o_ * 128:256 + (o_ + 1) * 128]
                                    ta3 = ta[:, 0:w_].rearrange("p (f c) -> p f c", c=128)
                                    tb3 = tb_[:, 0:w_].rearrange("p (f c) -> p f c", c=128)
                                    S.op("dve", lambda e: e.tensor_tensor(ta3, Ure, Kre, ALU.mult), r=PB(2, 0, w_) + [KSB], w=[TAB])
                                    S.op("dve", lambda e: e.tensor_tensor(tb3, Uim, Kim, ALU.mult), r=PB(3, 0, w_) + [KSB], w=[TBB])
                                    S.op("dve", lambda e: e.tensor_tensor(Yre[:, fs, :], ta3, tb3, ALU.subtract), r=[TAB, TBB], w=[YB])
                                    yield
                                    S.op("dve", lambda e: e.tensor_tensor(ta3, Ure, Kim, ALU.mult), r=PB(2, 0, w_) + [KSB], w=[TAB])
                                    S.op("dve", lambda e: e.tensor_tensor(tb3, Uim, Kre, ALU.mult), r=PB(3, 0, w_) + [KSB], w=[TBB])
                                    S.op("dve", lambda e: e.tensor_tensor(Yim[:, fs, :], ta3, tb3, ALU.add), r=[TAB, TBB], w=[YB])
                                    yield
                                for tt in range(nt):
                                    S.op("pe", lambda e: e.matmul(pbank[2][0:1, 0:128], nyqc[:, tt:tt + 1], u_tm[:, s_ * nt + tt, :], start=(tt == 0), stop=(tt == nt - 1)),
                                         r=[DB, UTB], w=PB(2, 0, 128), sig=(tt == nt - 1))
                                S.op("dve", lambda e: e.tensor_tensor(Ynq[:], pbank[2][0:1, 0:128], Knyq[0:1, o_ * 128:(o_ + 1) * 128], ALU.mult),
                                     r=PB(2, 0, 128) + [KSB], w=[YB])
                                yield
                                for c in range((L + 511) // 512):
                                    n_ = min(512, L - c * 512)
                                    bank = 4 + c % 2
                                    cs = slice(c * 512, c * 512 + n_)
                                    for ft in range(nt):
                                        S.op("pe", lambda e: e.matmul(pbank[bank][:, 0:n_], Yre[:, ft, :], Cm[:, ft, cs], start=(ft == 0), stop=False),
                                             r=[YB, DB], w=PB(bank), sig=False)
                                        S.op("pe", lambda e: e.matmul(pbank[bank][:, 0:n_], Yim[:, ft, :], Sn[:, ft, cs], start=False, stop=False),
                                             r=[YB, DB], w=PB(bank), sig=False)
                                    S.op("pe", lambda e: e.matmul(pbank[bank][:, 0:n_], Ynq[:], nyqr[0:1, cs], start=False, stop=True), r=[YB, DB], w=PB(bank))
                                    gs_ = slice(s_ * L + c * 512, s_ * L + c * 512 + n_)
                                    S.op("dve", lambda e: e.scalar_tensor_tensor(ta[:, 0:n_], src[:, gs_], hbias[:, o_, cb:cb + 1], pbank[bank][:, 0:n_], ALU.mult, ALU.add),
                                         r=[sB, CB] + PB(bank), w=[TAB])
                                    if o_ == 0:
                                        S.op("dve", lambda e: e.tensor_tensor(zz[:, gs_], ta[:, 0:n_], gate[:, gs_], ALU.mult), r=[TAB, gB], w=[ZB])
                                        yield
                                    else:
                                        S.op("dve", lambda e: e.tensor_tensor(yT[:, gs_], ta[:, 0:n_], gate[:, gs_], ALU.mult), r=[TAB, gB], w=[YTB])
                                        yield
                        if cb == 0:
                            dump(f"yT{pi}", yT[:, 0:128], [YTB])
                        if sample:
                            for g_ in range(NT):
                                S.op("pe", lambda e: e.transpose(pbf(6)[:, g_ * 128:(g_ + 1) * 128], yT[:, g_ * 128:(g_ + 1) * 128], ident_b[:]),
                                     r=[YTB, CB], w=PB(6), sig=(g_ == NT - 1))
                            S.op("act", lambda e: e.activation(u_tm[:].rearrange("p g c -> p (g c)"), pbf(6)[:, 0:NT * 128], AF.Copy), r=PB(6), w=[UTB])
                            yield
                            for g_ in range(NT):
                                S.op("pe", lambda e: e.matmul(pbank[2][:, 0:WIN], u_tm[:, g_, :], sel[:, g_, :], start=(g_ == 0), stop=(g_ == NT - 1)),
                                     r=[UTB, DB], w=PB(2), sig=(g_ == NT - 1))
                            S.op("act", lambda e: e.activation(yhyTw[:, cb, :], pbank[2][:, 0:WIN], AF.Copy), r=PB(2), w=[YHB])
                        else:
                            S.op("act", lambda e: e.activation(yhyTw[:, cb, :], yT[:], AF.Copy), r=[YTB], w=[YHB])
                        yield

                    if len(HWS) == 1:
                        for cb in range(8):
                            drive(hy_A(cb), None)
                            drive(None, hy_B(cb))
                    else:
                        drive(hy_A(0), None)
                        for cb in range(8):
                            drive(hy_A(cb + 1) if cb + 1 < 8 else None, hy_B(cb))
                    S.barrier()
                dump(f"yhyTw{pi}", yhyTw[:, 0, 0:128], [YHB])
                if stop_after == "hy":
                    return

                xmid = sb([128, len(tiles_w), D], F32, pr, side="right")
                XMB = [Buf(f"xmid{i}") for i in range(len(tiles_w))]
                h2 = sb([128, 16, Tw], BF16, pr, side="right")
                H2TB = Buf("h2")

                def x_rows_w():
                    if sample:
                        return [(din["xs_win"][r0:r0 + rr, :], rr, None) for (r0, rr) in tiles_w]
                    return [(din["xp"][prow0 + r0:prow0 + r0 + rr, :], rr, None) for (r0, rr) in tiles_w]

                def bc_rows(ph, gv, dst, dstB):
                    dg = sb([128, 128], F32, ph)
                    DGB = Buf("dg")
                    for kc in range(16):
                        S.op("dve", lambda e: e.tensor_scalar(dg[:], ident_f[:], gv[:, kc:kc + 1], None, ALU.mult), r=[CB, VB2], w=[DGB])
                        bank = 4 + (kc // 4) % 2
                        q = kc % 4
                        S.op("pe", lambda e: e.matmul(pbank[bank][:, q * 128:(q + 1) * 128], ones_f[:], dg[:], start=True, stop=True),
                             r=[CB, DGB], w=PB(bank, q * 128, (q + 1) * 128))
                        if q == 3:
                            S.op("act", lambda e: e.activation(dst[:, (kc - 3) * 128:(kc + 1) * 128], pbank[bank][:, :], AF.Copy), r=PB(bank), w=[dstB])

                late_vecs()
                with ExitStack() as ph:
                    if sample:
                        hTw = sb([128, 16, Tw], BF16, ph)
                        HWB = Buf("hTw")
                        with ExitStack() as ph2:
                            norm_T(ph2, x_rows_w(), hTw, HWB, gsm, shm, VB)
                            S.barrier()
                    else:
                        hTw, HWB = hT, HB
                    gm_bc = sb([128, D], F32, ph)
                    GMB = Buf("gm_bc")
                    bc_rows(ph, gvm, gm_bc, GMB)
                    mergedT = sb([128, 16, Tw], BF16, ph)
                    MGB = Buf("mergedT")
                    sg = [sb([128, Tw], F32, ph) for _ in range(4)]
                    tm = [sb([128, Tw], F32, ph) for _ in range(2)]
                    SGB = [Buf(f"sg{i}") for i in range(4)]
                    TMB = [Buf("tm0"), Buf("tm1")]
                    for n in range(16):
                        wt, wb = w_take("wmg", n)
                        wg = wt[:, 0:4096].rearrange("p (k n) -> p k n", n=256)
                        pb0 = (n % 2) * 4
                        for i_ in range(2):
                            sgi = (n % 2) * 2 + i_
                            for kc in range(16):
                                S.op("pe", lambda e: e.matmul(pbank[pb0 + i_][:, 0:Tw], wg[:, kc, i_ * 128:(i_ + 1) * 128], hTw[:, kc, :], start=(kc == 0), stop=(kc == 15)),
                                     r=[wb, HWB], w=PB(pb0 + i_), sig=(kc == 15))
                            S.op("act", lambda e: e.activation(sg[sgi][:], pbank[pb0 + i_][:, 0:Tw], AF.Sigmoid), r=PB(pb0 + i_), w=[SGB[sgi]])
                        wt2, wb2 = w_take("wmb", n)
                        wbh = wt2[:, 0:1024].rearrange("p (k n) -> p k n", n=128)
                        wbr = wt2[:, 1024:2048].rearrange("p (k n) -> p k n", n=128)
                        for i_, (wbx, ysrc, yB) in enumerate([(wbh, yhyTw, YHB), (wbr, yretTw, YRB)]):
                            sgi = (n % 2) * 2 + i_
                            for cc in range(8):
                                S.op("pe", lambda e: e.matmul(pbank[pb0 + 2 + i_][:, 0:Tw], wbx[:, cc, :], ysrc[:, cc, :], start=(cc == 0), stop=(cc == 7)),
                                     r=[wb2, yB], w=PB(pb0 + 2 + i_), sig=(cc == 7))
                            S.op("dve", lambda e: e.tensor_tensor(tm[i_][:], sg[sgi][:], pbank[pb0 + 2 + i_][:, 0:Tw], ALU.mult), r=[SGB[sgi]] + PB(pb0 + 2 + i_), w=[TMB[i_]])
                        S.op("dve", lambda e: e.tensor_tensor(mergedT[:, n, :], tm[0][:], tm[1][:], ALU.add), r=TMB, w=[MGB])
                    dump(f"mergedT{pi}", mergedT[:, 0, 0:128], [MGB])
                    junk = sb([128, 512], BF16, ph)
                    ssq = sb([128, len(tiles_w), 4], F32, ph)
                    rsm = sb([128, len(tiles_w)], F32, ph)
                    JB = Buf("junk"); SQB = Buf("ssq")
                    S.op("dve", lambda e: e.memset(ssq[:], 0.0), w=[SQB])
                    for cc in range(4):
                        base = (cc % 2) * 4
                        for half in range(2):
                            w0, wb0 = w_take("wout", cc * 2 + half)
                            wv0 = w0[:, 0:4096].rearrange("p (k n) -> p k n", n=512)
                            for ti, (r0, rr) in enumerate(tiles_w):
                                for k in range(8):
                                    kc = half * 8 + k
                                    S.op("pe", lambda e: e.matmul(pbank[base + ti][0:rr, :], mergedT[:, kc, r0:r0 + rr], wv0[:, k, :], start=(kc == 0), stop=(kc == 15)),
                                         r=[MGB, wb0], w=PB(base + ti), sig=(k == 7))
                        for ti, (r0, rr) in enumerate(tiles_w):
                            bank = base + ti
                            S.op("act", lambda e: e.activation(xmid[0:rr, ti, cc * 512:(cc + 1) * 512], pbank[bank][0:rr, :], AF.Copy), r=PB(bank), w=[XMB[ti]])
                            S.op("act", lambda e: e.activation(junk[0:rr, :], pbank[bank][0:rr, :], AF.Square, accum_out=ssq[0:rr, ti, cc:cc + 1]),
                                 r=PB(bank), w=[JB, SQB])
                    xin = [sb([128, D], F32, ph)]
                    XIB = [Buf("xi0")]
                    rows2 = []
                    for ti, (r0, rr) in enumerate(tiles_w):
                        b = 0
                        S.op("dve", lambda e: e.tensor_reduce(rsm[0:rr, ti:ti + 1], ssq[0:rr, ti, :], mybir.AxisListType.X, ALU.add), r=[SQB], w=[SQB])
                        rstd_from_ss(rsm[0:rr, ti:ti + 1], rsm[0:rr, ti:ti + 1], [SQB], [SQB], epsr[0:rr, :], 1.0 / D)
                        src = x_rows_w()[ti][0]
                        S.dma("sp", xin[b][0:rr, :], src, w=[XIB[b]])
                        S.op("dve", lambda e: e.scalar_tensor_tensor(xmid[0:rr, ti, :], xmid[0:rr, ti, :], rsm[0:rr, ti:ti + 1], gm_bc[0:rr, :], ALU.mult, ALU.mult),
                             r=[XMB[ti], SQB, GMB], w=[XMB[ti]])
                        S.op("dve", lambda e: e.tensor_tensor(xmid[0:rr, ti, :], xmid[0:rr, ti, :], xin[b][0:rr, :], ALU.add), r=[XMB[ti], XIB[b]], w=[XMB[ti]])
                        rows2.append((xmid[:, ti, :], rr, XMB[ti]))
                    dump(f"xmid{pi}", xmid[:, 0, 0:128], [XMB[0]])
                    with ExitStack() as ph2:
                        norm_T(ph2, rows2, h2, H2TB, gsf, shf, VB2)
                    if sample:
                        for wc, mc in ((0, 0), (WIN - 1, 1)):
                            S.op("dve", lambda e: e.tensor_scalar(h2[:, :, wc:wc + 1], h2[:, :, wc:wc + 1], hmask[:, mc:mc + 1], None, ALU.mult), r=[H2TB, CB], w=[H2TB])
                    S.barrier()
                pm.close()
                if stop_after == "mid":
                    return

                with ExitStack() as ph:
                    gf_bc = sb([128, D], F32, ph)
                    GFB = Buf("gf_bc")
                    bc_rows(ph, gvf, gf_bc, GFB)
                    actT = sb([128, NJ, Tw], BF16, ph)
                    ACB = Buf("actT")
                    p4a = ExitStack()
                    ph.callback(p4a.close)
                    ac = [sb([128, Tw], F32, p4a) for _ in range(2)]
                    bc_ = [sb([128, Tw], F32, p4a) for _ in range(2)]
                    gl = [sb([128, Tw], F32, p4a) for _ in range(2)]
                    ACB_ = [Buf("ac0"), Buf("ac1")]; BCB = [Buf("bc0"), Buf("bc1")]; GLB = [Buf("gl0"), Buf("gl1")]
                    seglen = WIN if sample else LP
                    nseg = Tw // seglen
                    for j in range(NJ):
                        wt, wb = w_take("wup", j)
                        wu = wt[:, 0:4096].rearrange("p (k n) -> p k n", n=256)
                        p_ = j % 2
                        for i_ in range(2):
                            bank = p_ * 2 + i_
                            for kc in range(16):
                                S.op("pe", lambda e: e.matmul(pbank[bank][:, 0:Tw], wu[:, kc, i_ * 128:(i_ + 1) * 128], h2[:, kc, :], start=(kc == 0), stop=(kc == 15)),
                                     r=[wb, H2TB], w=PB(bank), sig=(kc == 15))
                        conv3(ac[p_][:], pbank[p_ * 2][:, 0:Tw], fcw[:, j, :], nseg, seglen, PB(p_ * 2), [ACB_[p_]])
                        conv3(bc_[p_][:], pbank[p_ * 2 + 1][:, 0:Tw], fcw[:, NJ + j, :], nseg, seglen, PB(p_ * 2 + 1), [BCB[p_]])
                        S.op("act", lambda e: e.activation(gl[p_][:], ac[p_][:], AF.Gelu_apprx_tanh), r=[ACB_[p_]], w=[GLB[p_]])
                        S.op("dve", lambda e: e.tensor_tensor(actT[:, j, :], gl[p_][:], bc_[p_][:], ALU.mult), r=[GLB[p_], BCB[p_]], w=[ACB])
                    dump(f"actT{pi}", actT[:, 0, 0:128], [ACB])
                    S.barrier()
                    p4a.close()
                    fsb = sb([128, len(tiles_w), D], F32, ph)
                    FB = [Buf(f"f{i}") for i in range(len(tiles_w))]
                    junk = sb([128, 512], BF16, ph)
                    ssq = sb([128, len(tiles_w), 4], F32, ph)
                    rsm = sb([128, len(tiles_w)], F32, ph)
                    JB = Buf("junk"); SQB = Buf("ssq")
                    S.op("dve", lambda e: e.memset(ssq[:], 0.0), w=[SQB])
                    for cc in range(4):
                        base = (cc % 2) * 4
                        for jg in range(6):
                            njg = 8 if jg < 5 else 4
                            wt, wb = w_take("wdn", cc * 6 + jg)
                            wd = wt[:, 0:njg * 512].rearrange("p (j n) -> p j n", n=512)
                            for ti, (r0, rr) in enumerate(tiles_w):
                                for jj in range(njg):
                                    j = jg * 8 + jj
                                    last = (j == NJ - 1)
                                    S.op("pe", lambda e: e.matmul(pbank[base + ti][0:rr, :], actT[:, j, r0:r0 + rr], wd[:, jj, :], start=(j == 0), stop=last),
                                         r=[ACB, wb], w=PB(base + ti), sig=(jj == njg - 1))
                        for ti, (r0, rr) in enumerate(tiles_w):
                            S.op("act", lambda e: e.activation(fsb[0:rr, ti, cc * 512:(cc + 1) * 512], pbank[base + ti][0:rr, :], AF.Copy), r=PB(base + ti), w=[FB[ti]])
                            S.op("act", lambda e: e.activation(junk[0:rr, :], pbank[base + ti][0:rr, :], AF.Square, accum_out=ssq[0:rr, ti, cc:cc + 1]),
                                 r=PB(base + ti), w=[JB, SQB])
                    for ti, (r0, rr) in enumerate(tiles_w):
                        S.op("dve", lambda e: e.tensor_reduce(rsm[0:rr, ti:ti + 1], ssq[0:rr, ti, :], mybir.AxisListType.X, ALU.add), r=[SQB], w=[SQB])
                        rstd_from_ss(rsm[0:rr, ti:ti + 1], rsm[0:rr, ti:ti + 1], [SQB], [SQB], epsr[0:rr, :], 1.0 / D)
                        S.op("dve", lambda e: e.scalar_tensor_tensor(fsb[0:rr, ti, :], fsb[0:rr, ti, :], rsm[0:rr, ti:ti + 1], gf_bc[0:rr, :], ALU.mult, ALU.mult),
                             r=[FB[ti], SQB, GFB], w=[FB[ti]])
                        S.op("dve", lambda e: e.tensor_tensor(fsb[0:rr, ti, :], fsb[0:rr, ti, :], xmid[0:rr, ti, :], ALU.add), r=[FB[ti], XMB[ti]], w=[FB[ti]])
                        if sample:
                            if ti == 0:
                                S.dma("sp", ys[0:127, :], fsb[1:128, ti, :], r=[FB[ti]], w=[Buf()], is_out=True)
                            elif ti == 1:
                                S.dma("sp", ys[127:255, :], fsb[0:128, ti, :], r=[FB[ti]], w=[Buf()], is_out=True)
                            else:
                                S.dma("sp", ys[255:256, :], fsb[0:1, ti, :], r=[FB[ti]], w=[Buf()], is_out=True)
                        else:
                            S.dma("sp", yp[prow0 + r0:prow0 + r0 + rr, :], fsb[0:rr, ti, :], r=[FB[ti]], w=[Buf()], is_out=True)
                    S.barrier()

        for pi in passes:
            run_pass(pi)
            if stop_after is not None:
                break
        S.barrier()
        S.finish()
    return nc


_CACHE = {}


def kernel(**inputs):
    sh, per = host_prep(inputs)
    nc = build()
    in_maps = []
    for core in range(8):
        m = dict(sh)
        m.update(per[core])
        in_maps.append(m)
    res = run_bass_kernel_spmd(nc, in_maps, core_ids=list(range(8)))
    B, L_, D_ = inputs["x_prompt"].shape
    y_prompt = np.zeros((32, LP, D), np.float32)
    y_sample = np.zeros((2, LS, D), np.float32)
    new_state = np.zeros((32, 1, 2, 8, 128, 128), np.float32)
    for core in range(8):
        r = res.results[core]
        y_prompt[4 * core:4 * core + 4] = np.asarray(r["yp"]).reshape(4, LP, D)
        s, j = core // 4, core % 4
        y_sample[s, 256 * j:256 * (j + 1)] = np.asarray(r["ys"])
        new_state[4 * core:4 * core + 4, 0] = np.asarray(r["stout"])
    return (y_prompt, y_sample, new_state)
```
